# Optimizing a Trainium2 kernel written in Bass

```python
import math
import jax, jax.numpy as jnp
from jax import lax
import numpy as np


D_MODEL = 1024
BATCH = 2
SEQ = 8192
DEPTH = 1
DEC_BATCH = 32
DEC_SEQ = 32
PAST_LEN = 2048

CHUNK = 64
Q_BLOCK = 128
MIX_WIDTH = D_MODEL
ATTN_WIDTH = MIX_WIDTH // 2
POOL_WIDTH = MIX_WIDTH - ATTN_WIDTH
N_HEADS = 4
HEAD_DIM = ATTN_WIDTH // (2 * N_HEADS)
ROT_DIM = HEAD_DIM // 4
ROPE_THETA = 500000.0
POOL_WINDOWS = (2, 4, 8, 16)
N_POOL_GROUPS = len(POOL_WINDOWS)
POOL_GC = POOL_WIDTH // N_POOL_GROUPS
POOL_HIST = max(POOL_WINDOWS) - 1
IN_WIDTH = 4 * ATTN_WIDTH + 2 * POOL_WIDTH
NORM_EPS = 1e-6
SUBLN_EPS = 1e-5
NEG_INF = -1e30

kernel_name = "hybrid_diffattn_pool_stream_step"


def rmsnorm(x, g, eps=NORM_EPS):
    xf = x.astype(jnp.float32)
    y = xf * lax.rsqrt(jnp.mean(xf * xf, axis=-1, keepdims=True) + eps) * g.astype(jnp.float32)
    return y.astype(x.dtype)


def rope(x, pos):
    inv = ROPE_THETA ** (-jnp.arange(0, ROT_DIM, 2, dtype=jnp.float32) / ROT_DIM)
    ang = pos.astype(jnp.float32)[:, None] * inv[None, :]
    cos = jnp.cos(ang)[None, :, None, None, :]
    sin = jnp.sin(ang)[None, :, None, None, :]
    xf = x.astype(jnp.float32)
    a = xf[..., :ROT_DIM // 2]
    b = xf[..., ROT_DIM // 2:ROT_DIM]
    out = jnp.concatenate([a * cos - b * sin, b * cos + a * sin, xf[..., ROT_DIM:]], axis=-1)
    return out.astype(x.dtype)


def _diff_attn_block(qb, qpos_b, k, v, kpos, lam):
    s = jnp.einsum('bqhjd,bkhjd->bhjqk', qb.astype(jnp.float32), k.astype(jnp.float32)) * (HEAD_DIM ** -0.5)
    mask = (kpos[None, :] // CHUNK) <= (qpos_b[:, None] // CHUNK)
    s = jnp.where(mask[None, None, None], s, NEG_INF)
    p = jax.nn.softmax(s, axis=-1)
    a = p[:, :, 0] - lam * p[:, :, 1]
    return jnp.einsum('bhqk,bkhe->bqhe', a, v.astype(jnp.float32))


def diff_attention(q, k, v, qpos, kpos, lam):
    B, S = q.shape[0], q.shape[1]
    if S > Q_BLOCK and S % Q_BLOCK == 0:
        nb = S // Q_BLOCK
        qb = q.reshape(B, nb, Q_BLOCK, N_HEADS, 2, HEAD_DIM).transpose(1, 0, 2, 3, 4, 5)
        pb = qpos.reshape(nb, Q_BLOCK)
        ob = lax.map(lambda args: _diff_attn_block(args[0], args[1], k, v, kpos, lam), (qb, pb))
        return ob.transpose(1, 0, 2, 3, 4).reshape(B, S, N_HEADS, 2 * HEAD_DIM)
    return _diff_attn_block(q, qpos, k, v, kpos, lam)


def pool_mix(u, hist, pos, w_pool, pool_scale):
    B, S, C = u.shape
    uf = u.astype(jnp.float32)
    z = jnp.concatenate([hist.astype(jnp.float32), uf], axis=1)
    c = jnp.concatenate([jnp.zeros((B, 1, C), jnp.float32), jnp.cumsum(z, axis=1)], axis=1)
    end = c[:, POOL_HIST + 1:POOL_HIST + 1 + S]
    means = []
    for g, w in enumerate(POOL_WINDOWS):
        lo, hi = g * POOL_GC, (g + 1) * POOL_GC
        start = c[:, POOL_HIST + 1 - w:POOL_HIST + 1 - w + S, lo:hi]
        cnt = jnp.minimum(pos + 1, w).astype(jnp.float32)[None, :, None]
        means.append((end[..., lo:hi] - start) / cnt)
    m = jnp.concatenate(means, axis=-1) - uf
    m = jnp.einsum('bsgc,gcd->bsgd', m.reshape(B, S, N_POOL_GROUPS, POOL_GC),
                   w_pool.astype(jnp.float32)).reshape(B, S, C)
    return (m * pool_scale.astype(jnp.float32)).astype(u.dtype)


def hybrid_layer(x, pos, k_hist, v_hist, kpos_all, pool_hist, norm_g, w_in, lq1, lk1, lq2, lk2,
                 subln_g, w_pool, pool_scale, w_out, lam_init):
    B, S, _ = x.shape
    h = rmsnorm(x, norm_g)
    proj = jnp.einsum('bsd,de->bse', h, w_in)
    q, k, v, ga, u, gp = jnp.split(proj, [ATTN_WIDTH, 2 * ATTN_WIDTH, 3 * ATTN_WIDTH,
                                          4 * ATTN_WIDTH, 4 * ATTN_WIDTH + POOL_WIDTH], axis=-1)
    q = rope(q.reshape(B, S, N_HEADS, 2, HEAD_DIM), pos)
    k = rope(k.reshape(B, S, N_HEADS, 2, HEAD_DIM), pos)
    k_rows = k.reshape(B, S, N_HEADS, 2 * HEAD_DIM)
    v_rows = v.reshape(B, S, N_HEADS, 2 * HEAD_DIM)
    k_all = jnp.concatenate([k_hist, k_rows], axis=1)
    v_all = jnp.concatenate([v_hist, v_rows], axis=1)
    Lk = k_all.shape[1]
    lam = (jnp.exp(jnp.sum(lq1.astype(jnp.float32) * lk1.astype(jnp.float32)))
           - jnp.exp(jnp.sum(lq2.astype(jnp.float32) * lk2.astype(jnp.float32))) + lam_init)
    o = diff_attention(q, k_all.reshape(B, Lk, N_HEADS, 2, HEAD_DIM), v_all, pos, kpos_all, lam)
    o = rmsnorm(o, subln_g, SUBLN_EPS) * (1.0 - lam_init)
    a_out = o.reshape(B, S, ATTN_WIDTH).astype(x.dtype) * jax.nn.silu(ga)
    p_out = pool_mix(u, pool_hist, pos, w_pool, pool_scale) * jax.nn.silu(gp)
    y = x + jnp.einsum('bse,ed->bsd', jnp.concatenate([a_out, p_out], axis=-1), w_out)
    new_pool = jnp.concatenate([pool_hist.astype(u.dtype), u], axis=1)[:, -POOL_HIST:]
    return y, k_rows, v_rows, new_pool


def setup_inputs(seed: int = 0) -> dict:
    key = jax.random.key(seed)
    ks = jax.random.split(key, 18)
    nrm = jax.random.normal
    f32 = jnp.float32
    return {
        "x_prompt": nrm(ks[0], (BATCH, SEQ, D_MODEL), f32),
        "x_sample": nrm(ks[1], (DEC_BATCH, DEC_SEQ, D_MODEL), f32),
        "cache_k": nrm(ks[2], (DEPTH, DEC_BATCH, PAST_LEN, N_HEADS, 2 * HEAD_DIM), f32),
        "cache_v": nrm(ks[3], (DEPTH, DEC_BATCH, PAST_LEN, N_HEADS, 2 * HEAD_DIM), f32),
        "state_pool": nrm(ks[4], (DEPTH, DEC_BATCH, POOL_HIST, POOL_WIDTH), f32),
        "norm_g": 1.0 + 0.05 * nrm(ks[5], (DEPTH, D_MODEL), f32),
        "w_in": nrm(ks[6], (DEPTH, D_MODEL, IN_WIDTH), f32) * D_MODEL ** -0.5,
        "lambda_q1": 0.1 * nrm(ks[7], (DEPTH, HEAD_DIM), f32),
        "lambda_k1": 0.1 * nrm(ks[8], (DEPTH, HEAD_DIM), f32),
        "lambda_q2": 0.1 * nrm(ks[9], (DEPTH, HEAD_DIM), f32),
        "lambda_k2": 0.1 * nrm(ks[10], (DEPTH, HEAD_DIM), f32),
        "subln_g": 1.0 + 0.05 * nrm(ks[11], (DEPTH, 2 * HEAD_DIM), f32),
        "w_pool": nrm(ks[12], (DEPTH, N_POOL_GROUPS, POOL_GC, POOL_GC), f32) * POOL_GC ** -0.5,
        "pool_scale": 1.0 + 0.05 * nrm(ks[13], (DEPTH, POOL_WIDTH), f32),
        "w_out": nrm(ks[14], (DEPTH, MIX_WIDTH, D_MODEL), f32) * MIX_WIDTH ** -0.5,
        "final_g": 1.0 + 0.05 * nrm(ks[15], (D_MODEL,), f32),
    }


def reference(x_prompt, x_sample, cache_k, cache_v, state_pool, norm_g, w_in, lambda_q1, lambda_k1,
              lambda_q2, lambda_k2, subln_g, w_pool, pool_scale, w_out, final_g):
    B, S, _ = x_prompt.shape
    DB, DS, _ = x_sample.shape
    L = cache_k.shape[2]
    pos_p = jnp.arange(S, dtype=jnp.int32)
    pos_s = L + jnp.arange(DS, dtype=jnp.int32)
    kpos_s = jnp.concatenate([jnp.arange(L, dtype=jnp.int32), pos_s])
    xp, xs = x_prompt, x_sample
    kp_l, vp_l, pp_l, ks_l, vs_l, ps_l = [], [], [], [], [], []
    for l in range(DEPTH):
        lam_init = 0.8 - 0.6 * math.exp(-0.3 * l)
        shared = (norm_g[l], w_in[l], lambda_q1[l], lambda_k1[l], lambda_q2[l], lambda_k2[l],
                  subln_g[l], w_pool[l], pool_scale[l], w_out[l], lam_init)
        empty_kv = jnp.zeros((B, 0, N_HEADS, 2 * HEAD_DIM), xp.dtype)
        zero_pool = jnp.zeros((B, POOL_HIST, POOL_WIDTH), xp.dtype)
        xp, kp, vp, pp = hybrid_layer(xp, pos_p, empty_kv, empty_kv, pos_p, zero_pool, *shared)
        xs, kn, vn, pn = hybrid_layer(xs, pos_s, cache_k[l], cache_v[l], kpos_s, state_pool[l], *shared)
        kp_l.append(kp); vp_l.append(vp); pp_l.append(pp)
        ks_l.append(kn); vs_l.append(vn); ps_l.append(pn)
    y_prompt = rmsnorm(xp, final_g)
    y_sample = rmsnorm(xs, final_g)
    k_prompt = jnp.stack(kp_l, axis=0)
    v_prompt = jnp.stack(vp_l, axis=0)
    pool_prompt = jnp.stack(pp_l, axis=0)
    k_sample = jnp.stack(ks_l, axis=0)
    v_sample = jnp.stack(vs_l, axis=0)
    pool_sample = jnp.stack(ps_l, axis=0)
    return (y_prompt, y_sample, k_prompt, v_prompt, pool_prompt, k_sample, v_sample, pool_sample)
```

```python
import types
import numpy as np
import ml_dtypes
import concourse.bass as bass
import concourse.mybir as mybir
from concourse.bass_utils import run_bass_kernel_spmd

F32 = mybir.dt.float32
BF16 = mybir.dt.bfloat16
I32 = mybir.dt.int32
AF = mybir.ActivationFunctionType
ALU = mybir.AluOpType

D = 1024
NPR = 8192
NSM = 512
NT = NPR + NSM
CH = 512
NCH = NT // CH
NTILE = NT // 128
NQ = 2176
PAST = 2048
LAM_INIT = 0.2
POOL_WINDOWS = (2, 4, 8, 16)


def _freeze(fn):
    if fn.__closure__ is None:
        return fn
    cells = []
    for c in fn.__closure__:
        try:
            cells.append(types.CellType(c.cell_contents))
        except ValueError:
            cells.append(c)
    return types.FunctionType(fn.__code__, fn.__globals__, fn.__name__, fn.__defaults__, tuple(cells))


class Tok:
    __slots__ = ("kind", "eng", "h", "val")

    def __init__(self, kind, eng=None):
        self.kind, self.eng, self.h, self.val = kind, eng, None, None


class Prog:
    ENGS = ("pe", "act", "dve", "pool", "sp")

    def __init__(self, nc):
        self.nc = nc
        self.q = {e: [] for e in self.ENGS}
        self.sem = {e: nc.alloc_semaphore(name=f"s_{e}") for e in self.ENGS if e != "sp"}
        self.tick = {e: 0 for e in self.ENGS}
        self.waited = {e: {} for e in self.ENGS}
        self.dma_sems = []
        self.main = []
        self.cur = self.main
        self.ncoll = 0

    def section(self):
        return []

    def use(self, sec):
        self.cur = sec

    def include(self, sec):
        self.cur.append(("sec", sec))

    def op(self, eng, fn, deps=(), signal=True):
        tok = Tok("eng", eng) if signal else None
        self.cur.append(("op", eng, _freeze(fn), list(deps), tok))
        return tok

    def new_dma_sem(self, name):
        h = self.nc.alloc_semaphore(name=name)
        st = {"h": h, "v": 0}
        self.dma_sems.append(st)
        return st

    def dma(self, queue, st, out, in_, deps=(), **kw):
        tok = Tok("dma")
        self.cur.append(("dma", queue, st, out, in_, list(deps), kw, tok))
        return tok

    def coll(self, fn, deps=()):
        tok = Tok("dma")
        self.cur.append(("coll", _freeze(fn), list(deps), tok))
        return tok

    def wait_only(self, eng, deps):
        self.cur.append(("wait", eng, list(deps)))

    def barrier(self):
        self.cur.append(("barrier",))

    def wait_all_dma(self, eng):
        self.cur.append(("waitall", eng))

    def _emit_waits(self, eng, deps):
        out = []
        best = {}
        for d in deps:
            if d is None:
                continue
            if isinstance(d, Tok):
                assert d.val is not None, "dependency flushed after its consumer"
                k_ = ("e", d.eng) if d.kind == "eng" else ("d", id(d.h))
                v_ = d.val
            else:
                k_ = ("d", id(d[1])); v_ = d[2]
            if k_ not in best or v_ > best[k_][0]:
                best[k_] = (v_, d)
        for _, d in best.values():
            if isinstance(d, Tok):
                assert d.val is not None, "dependency flushed after its consumer"
                if d.kind == "eng":
                    if d.eng == eng and eng == "pe":
                        continue
                    key = ("eng", d.eng); h = self.sem[d.eng]
                else:
                    key = ("dma", id(d.h)); h = d.h
                v = d.val
            else:
                key = ("dma", id(d[1])); h = d[1]; v = d[2]
            if self.waited[eng].get(key, 0) >= v:
                continue
            self.waited[eng][key] = v
            out.append((h, v))
        return out

    def _all_toks(self):
        toks = []
        for e in ("pe", "act", "dve", "pool"):
            if self.tick[e] > 0:
                t = Tok("eng", e); t.val = self.tick[e]
                toks.append(t)
        toks += [("dma", st["h"], st["v"]) for st in self.dma_sems if st["v"] > 0]
        return toks

    def _push_wait(self, eng, deps):
        waits = self._emit_waits(eng, deps)
        if waits:
            def run(e, waits=waits):
                for h, v in waits:
                    e.wait_ge(h, v)
            self.q[eng].append(run)

    def _flush(self, sec):
        for rec in sec:
            kind = rec[0]
            if kind == "sec":
                self._flush(rec[1])
            elif kind == "op":
                _, eng, fn, deps, tok = rec
                waits = self._emit_waits(eng, deps)
                sem = self.sem[eng]
                if tok is not None:
                    self.tick[eng] += 1
                    tok.val = self.tick[eng]

                def run(e, waits=waits, fn=fn, sig=(tok is not None), sem=sem):
                    for h, v in waits:
                        e.wait_ge(h, v)
                    ins = fn(e)
                    if sig:
                        ins.then_inc(sem, 1)
                self.q[eng].append(run)
            elif kind == "dma":
                _, queue, st, out, in_, deps, kw, tok = rec
                waits = self._emit_waits(queue, deps)
                st["v"] += 16
                tok.h, tok.val = st["h"], st["v"]

                def run(e, waits=waits, out=out, in_=in_, kw=kw, h=st["h"]):
                    for hh, v in waits:
                        e.wait_ge(hh, v)
                    e.dma_start(out=out, in_=in_, **kw).then_inc(h, 16)
                self.q[queue].append(run)
            elif kind == "coll":
                _, fn, deps, tok = rec
                waits = self._emit_waits("pool", deps)
                h = self.nc.alloc_semaphore(name=f"cc_sem{self.ncoll}")
                self.ncoll += 1
                tok.h, tok.val = h, 1

                def run(e, waits=waits, fn=fn, h=h):
                    for hh, v in waits:
                        e.wait_ge(hh, v)
                    fn(e).then_inc(h)
                self.q["pool"].append(run)
            elif kind == "wait":
                self._push_wait(rec[1], rec[2])
            elif kind == "barrier":
                toks = self._all_toks()
                for e in self.ENGS:
                    self._push_wait(e, toks)
            elif kind == "waitall":
                self._push_wait(rec[1], [("dma", st["h"], st["v"]) for st in self.dma_sems if st["v"] > 0])

    def finish(self):
        nc = self.nc
        self._flush(self.main)
        with nc.Block() as block:
            @block.tensor
            def _(e):
                for f in self.q["pe"]:
                    f(e)

            @block.scalar
            def _(e):
                for f in self.q["act"]:
                    f(e)

            @block.vector
            def _(e):
                for f in self.q["dve"]:
                    f(e)

            @block.gpsimd
            def _(e):
                for f in self.q["pool"]:
                    f(e)

            @block.sync
            def _(e):
                for f in self.q["sp"]:
                    f(e)


class FinG:
    def __init__(self, gen, res, t, qi):
        self.gen, self.res, self.t, self.qi, self.age = gen, res, t, qi, 0

    def __next__(self):
        return next(self.gen)


class Ring:
    def __init__(self, bufs):
        self.bufs = bufs
        self.free = [[] for _ in bufs]
        self.i = -1

    def next(self):
        self.i = (self.i + 1) % len(self.bufs)
        deps = self.free[self.i]
        self.free[self.i] = []
        return self.bufs[self.i], deps, self.i

    def release(self, idx, *toks):
        self.free[idx].extend(t for t in toks if t is not None)


class Bump:
    def __init__(self, slab):
        self.slab, self.off, self.n = slab, 0, slab.shape[1]

    def take(self, shape, dt):
        assert shape[0] == 128
        cnt = 1
        for d in shape[1:]:
            cnt *= d
        nw = cnt if dt == F32 else (cnt + 1) // 2
        nw = (nw + 7) // 8 * 8
        assert self.off + nw <= self.n, ("slab overflow", self.off, nw, self.n)
        v = self.slab[:, self.off:self.off + nw]
        self.off += nw
        if dt != F32:
            v = v.bitcast(dt)
        v = v[:, 0:cnt]
        if len(shape) == 2:
            return v
        names = " ".join(f"d{i}" for i in range(len(shape) - 1))
        kw = {f"d{i}": shape[i + 1] for i in range(len(shape) - 2)}
        return v.rearrange(f"p ({names}) -> p {names}", **kw)


def sb(nc, name, shape, dt):
    return nc.alloc_sbuf_tensor("sb_" + name, shape, dt).ap()


def emit_phase_ab(nc, P, io, nchunks_attn=16, do_sample=True):
    xT, w_tm, w_fm = io["xT"], io["w_tm"], io["w_fm"]

    big0 = nc.alloc_psum_tensor("big0", [128, 2048], F32).ap()
    big1 = nc.alloc_psum_tensor("big1", [128, 2048], F32).ap()

    def bank(k):
        bg = big0 if k < 4 else big1
        return bg[:, 512 * (k % 4):512 * (k % 4) + 512]

    QT = sb(nc, "QT", [128, NT], BF16)
    KT = sb(nc, "KT", [128, NT], BF16)
    V1 = sb(nc, "V1", [128, NTILE, 132], BF16)
    GA = sb(nc, "GA", [128, NTILE, 128], BF16)
    ident = sb(nc, "ident", [128, 128], BF16)
    identf = sb(nc, "identf", [128, 128], F32)
    ones_bf = sb(nc, "ones_bf", [128, 128], BF16)
    one_f = sb(nc, "one_f", [1, 1], F32)
    Wtm = sb(nc, "Wtm", [128, 8, 512], BF16)
    Wfm = sb(nc, "Wfm", [128, 8, 256], BF16)
    Wp = sb(nc, "Wp", [128, 128], BF16)
    wp32 = sb(nc, "wp32", [128, 128], F32)
    gcol = sb(nc, "gcol", [128, 8], F32)
    subln = sb(nc, "subln", [128, 128], F32)
    pscale = sb(nc, "pscale", [128, 1], F32)
    meta = sb(nc, "meta", [128, 32], F32)
    lamv = sb(nc, "lamv", [128, 4, 64], F32)
    lamt = sb(nc, "lamt", [128, 2, 64], F32)
    lams = sb(nc, "lams", [128, 2], F32)
    neglam = sb(nc, "neglam", [128, 1], F32)
    CC = sb(nc, "CC", [128, NTILE, 16], F32)
    SS = sb(nc, "SS", [128, NTILE, 16], F32)

    cst = P.new_dma_sem("cst")
    t_c = []
    t_c.append(P.dma("sp", cst, gcol, io["gcol"]))
    t_c.append(P.dma("sp", cst, subln, io["subln"]))
    t_c.append(P.dma("sp", cst, pscale, io["pscale"]))
    t_c.append(P.dma("sp", cst, meta, io["meta"]))
    t_c.append(P.dma("sp", cst, lamv, io["lamv"]))
    t_c.append(P.dma("sp", cst, wp32, io["wpool"]))
    t_c.append(P.dma("sp", cst, CC, io["cc"].rearrange("(n p) e -> p n e", p=128)))
    t_c.append(P.dma("sp", cst, SS, io["ss"].rearrange("(n p) e -> p n e", p=128)))
    t_cst = t_c[-1]

    t_idf = P.op("pool", lambda e: e.memset(identf, 0.0))
    t_idf = P.op("pool", lambda e: e.affine_select(identf, identf, [[-1, 128]], ALU.not_equal, 1.0,
                                                   base=0, channel_multiplier=1), deps=[t_idf])
    t_id = P.op("dve", lambda e: e.tensor_copy(ident, identf), deps=[t_idf])
    t_ones = P.op("pool", lambda e: e.memset(ones_bf, 1.0))
    t_onef = P.op("pool", lambda e: e.memset(one_f, 1.0))
    t_v1 = P.op("pool", lambda e: e.memset(V1[:, :, 128:132], 0.0))
    t_v1 = P.op("pool", lambda e: e.memset(V1[:, :, 128:129], 1.0), deps=[t_v1])
    t_wp = P.op("dve", lambda e: e.tensor_copy(Wp, wp32), deps=[t_cst])
    t_sub = P.op("dve", lambda e: e.tensor_scalar(subln, subln, 1.0 - LAM_INIT, None, ALU.mult), deps=[t_cst])
    t_l = P.op("dve", lambda e: e.tensor_tensor(lamt, lamv[:, 0:4:2, :], lamv[:, 1:4:2, :], ALU.mult), deps=[t_cst])
    t_l = P.op("dve", lambda e: e.tensor_reduce(lams, lamt, mybir.AxisListType.X, ALU.add), deps=[t_l])
    t_l = P.op("act", lambda e: e.activation(lams, lams, AF.Exp), deps=[t_l])
    t_lam = P.op("dve", lambda e: e.scalar_tensor_tensor(neglam, lams[:, 1:2], -LAM_INIT, lams[:, 0:1],
                                                        ALU.add, ALU.subtract), deps=[t_l])

    R32 = sb(nc, "R32", [128, 12288], F32)

    def carve(off, n, dt=F32):
        v = R32[:, off:off + n]
        return v if dt == F32 else v.bitcast(dt)

    xs0 = carve(0, 4096).rearrange("p (k n) -> p k n", k=8)
    xs1 = sb(nc, "xs1", [128, 8, CH], F32)
    xs = xs0
    xss = [xs0, xs1]
    xb = [carve(4096 + 2048 * i, 2048, BF16).rearrange("p (k n) -> p k n", k=8) for i in range(2)]
    sq = carve(8192, 2048, BF16).rearrange("p (k n) -> p k n", k=8)
    R2 = sb(nc, "R2", [128, 12032], F32)
    a2 = Bump(R2)
    lnv = a2.take([128, CH], F32)
    rstd_bcs = [a2.take([128, CH], F32) for i in range(2)]
    rstd_col = a2.take([128, 4], F32)
    tm = carve(10240, 2048).rearrange("p (i n) -> p i n", i=4)
    qkb = a2.take([128, 4, 256], BF16)
    rt1 = a2.take([128, 4, 4, 16], F32)
    rt2 = a2.take([128, 4, 4, 16], F32)
    sga = a2.take([128, 4, 128], F32)
    UH = [a2.take([128, 16 + CH], F32) for i in range(2)]
    UHs = a2.take([128, 16, 48], F32)
    gpT = a2.take([128, CH], F32)
    sgp = a2.take([128, CH], F32)
    s2e = a2.take([128, 16 * 46], F32)
    s4e = a2.take([128, 16 * 44], F32)
    s8e = a2.take([128, 16 * 40], F32)
    s16 = a2.take([128, CH], F32)
    sel = a2.take([128, CH], F32)
    m_bf = a2.take([128, CH], BF16)
    pst = [a2.take([128, CH], BF16) for i in range(2)]
    UTp = a2.take([128, 128], F32)
    UTs = a2.take([128, 4, 128], F32)
    ucs = a2.take([128, CH], F32)

    wsem = P.new_dma_sem("wsem")
    t_w = P.dma("sp", wsem, xs[:, :, 0:512], w_tm.rearrange("(kc p) n -> p kc n", p=128))
    t_wt = None
    for kc in range(8):
        t_wt = P.op("dve", lambda e, kc=kc: e.tensor_scalar(Wtm[:, kc, :], xs[:, kc, :], gcol[:, kc:kc + 1], None, ALU.mult),
                    deps=[t_w, t_cst])
    t_w2 = P.dma("sp", wsem, xs[:, :, 0:256], w_fm.rearrange("(kc p) n -> p kc n", p=128), deps=[t_wt])
    for kc in range(8):
        t_wt = P.op("dve", lambda e, kc=kc: e.tensor_scalar(Wfm[:, kc, :], xs[:, kc, 0:256], gcol[:, kc:kc + 1], None, ALU.mult),
                    deps=[t_w2])
    t_wdone = t_wt

    t_uh0 = P.op("pool", lambda e: e.memset(UH[0][:, 0:16], 0.0))
    t_uhs0 = P.op("pool", lambda e: e.memset(UHs[:, :, 0:1], 0.0))
    spsem = P.new_dma_sem("spsem")
    t_sp = P.dma("sp", spsem, UHs[:, :, 1:16], io["spT"], deps=[t_uhs0])

    xsem = P.new_dma_sem("xsem")
    xsem2 = [P.new_dma_sem(f"xsem2_{i}") for i in range(2)]
    kvsem = P.new_dma_sem("kvsem")
    psem = [P.new_dma_sem(f"psem{i}") for i in range(2)]
    asem = [P.new_dma_sem(f"asem{i}") for i in range(2)]
    fsem = P.new_dma_sem("fsem")
    xTv = xT.rearrange("(kc p) n -> p kc n", p=128)
    k_out_v = io["k_out"].rearrange("(p n) e -> p n e", p=128)
    v_out_v = io["v_out"].rearrange("(p n) e -> p n e", p=128)
    exa_dst = io["exa_dst"]
    exp_dst = io["exp_dst"]

    psT = bank(4).bitcast(BF16).rearrange("p (w i e) -> p w i e", w=2, i=4)
    ps_ss = bank(0)
    ps_col = bank(1)[:, 0:4]
    ps_tm = [bank(2), bank(3)]
    ps_fm = [bank(5), bank(6)]
    ps_pool = bank(7)

    t_xfree = [t_wdone]
    xb_free = [[], []]
    tm_free = []
    qkb_free = []
    pst_free = [[], []]
    uh_tok = [t_uh0, None]
    ps_tm_free = [[], []]
    ps_fm_free = [[], []]
    psT_free = []
    ps_ss_free = []
    ps_col_free = []
    ps_pool_free = []
    rstd_free = []
    sq_free = []
    pool_tmp_free = []
    out_toks = []
    t_ktqt = None

    main_sec = P.cur
    SA0 = [P.section() for _ in range(NCH)]
    S1a = [P.section() for _ in range(NCH)]
    SA2 = [P.section() for _ in range(NCH)]
    S1b = [P.section() for _ in range(NCH)]
    SB2 = [P.section() for _ in range(NCH)]
    S2a = [P.section() for _ in range(NCH)]
    SC2 = [P.section() for _ in range(NCH)]
    S2b = [P.section() for _ in range(NCH)]
    rstd_free2 = [[], []]
    for t in range(NCH):
        is_s = (t == NCH - 1)
        xbt = xb[t % 2]
        rstd_bc = rstd_bcs[t % 2]
        rstd_free = rstd_free2[t % 2]
        P.use(SA0[t])
        xs = xss[t % 2]
        if t == 0:
            t_x = P.dma("sp", xsem2[0], xs, xTv[:, :, 0:CH], deps=t_xfree)
            t_xnext = P.dma("sp", xsem2[1], xss[1], xTv[:, :, CH:2 * CH])
        else:
            t_x = t_xnext
        t_cast = P.op("dve", lambda e, xbt=xbt: e.tensor_copy(xbt, xs), deps=[t_x] + xb_free[t % 2])
        t_sq = P.op("act", lambda e: e.activation(sq, xs, AF.Square), deps=[t_x] + sq_free)
        t_xfree = [t_cast, t_sq]
        if t >= 1:
            t_xnext = t_xnext2
        if t + 2 < NCH:
            t_xnext2 = P.dma("sp", xsem2[t % 2], xs, xTv[:, :, CH * (t + 2):CH * (t + 3)], deps=t_xfree)
        P.use(S1a[t])
        t_ss = None
        for kc in range(8):
            t_ss = P.op("pe", lambda e, kc=kc: e.matmul(ps_ss, lhsT=ones_bf, rhs=sq[:, kc, :], start=(kc == 0), stop=(kc == 7)),
                        deps=[t_sq, t_ones] + ps_ss_free, signal=(kc == 7))
        sq_free = [t_ss]
        t_ln = P.op("act", lambda e: e.activation(lnv, ps_ss, AF.Ln, bias=1e-6, scale=1.0 / D), deps=[t_ss] + rstd_free)
        ps_ss_free = [t_ln]
        t_rs = P.op("act", lambda e: e.activation(rstd_bc, lnv, AF.Exp, scale=-0.5), deps=[t_ln])
        P.use(SA2[t])
        t_cm = None
        for i in range(4):
            t_cm = P.op("pe", lambda e, i=i: e.matmul(ps_col[:, i:i + 1], lhsT=rstd_bc[0:1, 128 * i:128 * i + 128], rhs=one_f,
                                                      start=True, stop=True),
                        deps=[t_rs, t_onef] + ps_col_free, signal=(i == 3))
        t_rc = P.op("dve", lambda e: e.tensor_copy(rstd_col, ps_col), deps=[t_cm] + tm_free)
        ps_col_free = [t_rc]
        P.use(S1b[t])
        t_ev = []
        for i in range(4):
            pt = ps_tm[i % 2]
            t_mm = None
            for kc in range(8):
                t_mm = P.op("pe", lambda e, i=i, kc=kc, pt=pt: e.matmul(pt, lhsT=xbt[:, kc, 128 * i:128 * i + 128], rhs=Wtm[:, kc, :],
                                                                     start=(kc == 0), stop=(kc == 7)),
                            deps=[t_cast, t_wdone] + ps_tm_free[i % 2], signal=(kc == 7))
            te = P.op("act", lambda e, i=i, pt=pt: e.activation(tm[:, i, :], pt, AF.Copy, scale=rstd_col[:, i:i + 1]),
                      deps=[t_mm, t_rc] + tm_free)
            ps_tm_free[i % 2] = [te]
            t_ev.append(te)
        t_tm = t_ev[-1]
        qk = tm[:, :, 0:256].rearrange("p i (g d) -> p i g d", g=4)
        cc_b = CC[:, 4 * t:4 * t + 4, :].unsqueeze(2).to_broadcast([128, 4, 4, 16])
        ss_b = SS[:, 4 * t:4 * t + 4, :].unsqueeze(2).to_broadcast([128, 4, 4, 16])
        t_r1 = P.op("dve", lambda e: e.tensor_tensor(rt1, qk[:, :, :, 0:16], cc_b, ALU.mult), deps=[t_tm, t_cst])
        t_r2 = P.op("dve", lambda e: e.tensor_tensor(rt2[:, :, :, 0:8], qk[:, :, :, 8:16], ss_b[:, :, :, 0:8], ALU.mult), deps=[t_tm])
        t_r3 = P.op("dve", lambda e: e.tensor_tensor(rt2[:, :, :, 8:16], qk[:, :, :, 0:8], ss_b[:, :, :, 8:16], ALU.mult), deps=[t_r2])
        t_rope = P.op("dve", lambda e: e.tensor_tensor(qk[:, :, :, 0:16], rt1, rt2, ALU.add), deps=[t_r1, t_r3])
        t_ko = P.dma("sp", kvsem, k_out_v[:, 4 * t:4 * t + 4, :], tm[:, :, 128:256], deps=[t_rope])
        t_vo = P.dma("sp", kvsem, v_out_v[:, 4 * t:4 * t + 4, :], tm[:, :, 256:384], deps=[t_tm])
        t_qb = P.op("dve", lambda e: e.tensor_scalar(qkb[:, :, 0:128], tm[:, :, 0:128], 0.125, None, ALU.mult), deps=[t_rope] + qkb_free)
        t_kb = P.op("dve", lambda e: e.tensor_copy(qkb[:, :, 128:256], tm[:, :, 128:256]), deps=[t_qb])
        t_vb = P.op("dve", lambda e: e.tensor_copy(V1[:, 4 * t:4 * t + 4, 0:128], tm[:, :, 256:384]), deps=[t_tm])
        t_sg = P.op("act", lambda e: e.activation(sga, tm[:, :, 384:512], AF.Silu), deps=[t_tm])
        t_ga = P.op("dve", lambda e: e.tensor_tensor(GA[:, 4 * t:4 * t + 4, :], sga,
                                                      subln.unsqueeze(1).to_broadcast([128, 4, 128]), ALU.mult), deps=[t_sg, t_sub])
        tm_free = [t_ko, t_vo, t_kb, t_vb, t_sg]
        P.use(SB2[t])
        t_tr = None
        for w in range(2):
            for i in range(4):
                t_tr = P.op("pe", lambda e, w=w, i=i: e.transpose(psT[:, w, i, :], qkb[:, i, 128 * w:128 * w + 128], ident),
                            deps=[t_kb, t_id] + psT_free, signal=(w == 1 and i == 3))
        qkb_free = [t_tr]
        t_q = P.op("dve", lambda e: e.tensor_copy(QT[:, CH * t:CH * (t + 1)], psT[:, 0].rearrange("p i e -> p (i e)")), deps=[t_tr])
        t_k = P.op("dve", lambda e: e.tensor_copy(KT[:, CH * t:CH * (t + 1)], psT[:, 1].rearrange("p i e -> p (i e)")), deps=[t_q])
        psT_free = [t_k]
        t_ktqt = t_k
        P.use(S2a[t])
        t_fm = []
        for w in range(2):
            pf = ps_fm[w]
            t_mm = None
            for kc in range(8):
                t_mm = P.op("pe", lambda e, w=w, kc=kc, pf=pf: e.matmul(pf, lhsT=Wfm[:, kc, 128 * w:128 * w + 128], rhs=xbt[:, kc, :],
                                                                     start=(kc == 0), stop=(kc == 7)),
                            deps=[t_cast, t_wdone] + ps_fm_free[w], signal=(kc == 7))
            t_fm.append(t_mm)
        xb_free[t % 2] = [t_fm[1]]
        P.use(SC2[t])
        if not is_s:
            uh = UH[t % 2]
            S_, L_ = 1, CH
            uview = uh.rearrange("p (s l) -> p s l", s=1)
            t_u = P.op("dve", lambda e, uh=uh: e.tensor_tensor(uh[:, 16:16 + CH], ps_fm[0], rstd_bc, ALU.mult),
                       deps=[t_fm[0], t_rs, uh_tok[t % 2]] + pool_tmp_free)
        else:
            S_, L_ = 16, 32
            uview = UHs
            t_u = P.op("dve", lambda e: e.tensor_tensor(UHs[:, :, 16:48], ps_fm[0].rearrange("p (s l) -> p s l", s=16),
                                                        rstd_bc.rearrange("p (s l) -> p s l", s=16), ALU.mult),
                       deps=[t_fm[0], t_rs, t_sp] + pool_tmp_free)
        t_g = P.op("dve", lambda e: e.tensor_tensor(gpT, ps_fm[1], rstd_bc, ALU.mult), deps=[t_fm[1], t_rs] + pool_tmp_free)
        ps_fm_free = [[t_u], [t_g]]
        rstd_free2[t % 2] = [t_g, t_u, t_cm]
        t_sgp = P.op("act", lambda e: e.activation(sgp, gpT, AF.Silu), deps=[t_g])
        if t + 1 < NCH - 1:
            uh_tok[(t + 1) % 2] = P.op("dve", lambda e, t=t: e.tensor_copy(UH[(t + 1) % 2][:, 0:16], UH[t % 2][:, CH:CH + 16]), deps=[t_u])
        W2 = 14 + L_; W4 = 12 + L_; W8 = 8 + L_
        v2 = s2e[:, 0:S_ * W2].rearrange("p (s l) -> p s l", s=S_)
        v4 = s4e[:, 0:S_ * W4].rearrange("p (s l) -> p s l", s=S_)
        v8 = s8e[:, 0:S_ * W8].rearrange("p (s l) -> p s l", s=S_)
        v16 = s16.rearrange("p (s l) -> p s l", s=S_)
        vsel = sel.rearrange("p (s l) -> p s l", s=S_)
        vm = m_bf.rearrange("p (s l) -> p s l", s=S_)
        t_p = P.op("dve", lambda e: e.tensor_tensor(v2, uview[:, :, 2:16 + L_], uview[:, :, 1:15 + L_], ALU.add), deps=[t_u])
        t_p = P.op("dve", lambda e: e.tensor_tensor(v4, v2[:, :, 2:W2], v2[:, :, 0:W2 - 2], ALU.add), deps=[t_p])
        t_p = P.op("dve", lambda e: e.tensor_tensor(v8, v4[:, :, 4:W4], v4[:, :, 0:W4 - 4], ALU.add), deps=[t_p])
        t_p = P.op("dve", lambda e: e.tensor_tensor(v16, v8[:, :, 8:W8], v8[:, :, 0:W8 - 8], ALU.add), deps=[t_p])
        t_p = P.op("dve", lambda e: e.tensor_scalar(vsel, v2[:, :, 14:W2], meta[:, 0:1], None, ALU.mult), deps=[t_p, t_cst])
        t_p = P.op("dve", lambda e: e.scalar_tensor_tensor(vsel, v4[:, :, 12:W4], meta[:, 1:2], vsel, ALU.mult, ALU.add), deps=[t_p])
        t_p = P.op("dve", lambda e: e.scalar_tensor_tensor(vsel, v8[:, :, 8:W8], meta[:, 2:3], vsel, ALU.mult, ALU.add), deps=[t_p])
        t_p = P.op("dve", lambda e: e.scalar_tensor_tensor(vsel, v16, meta[:, 3:4], vsel, ALU.mult, ALU.add), deps=[t_p])
        if t == 0:
            t_p = P.op("dve", lambda e: e.tensor_tensor(sel[:, 0:16], sel[:, 0:16], meta[:, 8:24], ALU.mult), deps=[t_p])
        t_m = P.op("dve", lambda e: e.scalar_tensor_tensor(vm, vsel, meta[:, 4:5], uview[:, :, 16:16 + L_], ALU.mult, ALU.subtract),
                   deps=[t_p] + ps_pool_free)
        pool_tmp_free = [t_m]
        P.use(S2b[t])
        t_pm = P.op("pe", lambda e: e.matmul(ps_pool, lhsT=Wp, rhs=m_bf, start=True, stop=True), deps=[t_m, t_wp] + ps_pool_free)
        pso = pst[t % 2]
        t_po = P.op("dve", lambda e, pso=pso: e.scalar_tensor_tensor(pso, ps_pool, pscale[:, 0:1], sgp, ALU.mult, ALU.mult),
                    deps=[t_pm, t_sgp, t_cst] + pst_free[t % 2])
        ps_pool_free = [t_po, t_pm]
        t_pd = P.dma("sp", psem[t % 2], exp_dst(t), pso, deps=[t_po])
        pst_free[t % 2] = [t_pd]
        if t == NCH - 2:
            t_trp = P.op("pe", lambda e, t=t: e.matmul(ps_pool[:, 0:128], lhsT=UH[t % 2][:, 16 + 384:16 + 512], rhs=identf, start=True, stop=True),
                         deps=[t_u, t_idf] + ps_pool_free)
            t_cpo = P.op("act", lambda e: e.copy(UTp, ps_pool[:, 0:128]), deps=[t_trp])
            ps_pool_free = ps_pool_free + [t_cpo]
            out_toks.append(P.dma("sp", fsem, io["pool_out"][16], UTp[113:128, :], deps=[t_cpo]))
        if is_s:
            t_ucs = P.op("act", lambda e: e.copy(ucs.rearrange("p (s l) -> p s l", s=16), UHs[:, :, 16:48]), deps=[t_u])
            t_trp = None
            for g4 in range(4):
                t_trp = P.op("pe", lambda e, g4=g4: e.matmul(ps_pool[:, 128 * g4:128 * g4 + 128], lhsT=ucs[:, 128 * g4:128 * g4 + 128], rhs=identf, start=True, stop=True),
                             deps=[t_ucs, t_idf] + ps_pool_free, signal=(g4 == 3))
            t_cpo = P.op("act", lambda e: e.copy(UTs, ps_pool.rearrange("p (g c) -> p g c", g=4)), deps=[t_trp])
            ps_pool_free = ps_pool_free + [t_cpo]
            for s_ in range(16):
                k4, g4 = s_ % 4, s_ // 4
                out_toks.append(P.dma("sp", fsem, io["pool_out"][s_], UTs[32 * k4 + 17:32 * k4 + 32, g4, :], deps=[t_cpo]))
    P.use(main_sec)
    P.include(SA0[0]); P.include(S1a[0]); P.include(SA2[0]); P.include(SA0[1]); P.include(S1b[0])
    for t in range(1, NCH):
        P.include(S1a[t]); P.include(SB2[t - 1]); P.include(S2a[t - 1]); P.include(SA2[t])
        P.include(SC2[t - 1])
        if t + 1 < NCH:
            P.include(SA0[t + 1])
        P.include(S1b[t]); P.include(S2b[t - 1])
    L_ = NCH - 1
    P.include(SB2[L_]); P.include(S2a[L_]); P.include(SC2[L_]); P.include(S2b[L_])
    t_A_done = [t_ktqt, t_ga, t_vb]
    P.barrier()

    NFIN = 3
    fin_o1 = [sb(nc, f"fin_o1_{i}", [128, 128], F32) for i in range(NFIN)]
    fin_o = [sb(nc, f"fin_o_{i}", [128, 128], F32) for i in range(NFIN)]
    fin_sq = sb(nc, "fin_sq", [128, 128], F32)
    fin_rz = [sb(nc, f"fin_rz_{i}", [128, 4], F32) for i in range(NFIN)]
    fin_ss = [sb(nc, f"fin_ss_{i}", [128, 2], F32) for i in range(NFIN)]
    fin_state = {"n": 0, "free": [[] for _ in range(NFIN)]}

    def finalize(rows, O1, O2, ga_ap, dst_ap, deps, dst_deps, res):
        R = slice(0, rows)
        k = fin_state["n"] % NFIN
        fin_state["n"] += 1
        o1, o, rz, ss_ = fin_o1[k], fin_o[k], fin_rz[k], fin_ss[k]
        d0 = list(deps) + fin_state["free"][k]
        t1 = P.op("dve", lambda e: e.reciprocal(rz[R, 0:1], O1[:, 128:129]), deps=d0)
        t2 = P.op("dve", lambda e: e.reciprocal(rz[R, 1:2], O2[:, 128:129]), deps=d0)
        t3 = P.op("dve", lambda e: e.tensor_tensor(rz[R, 2:3], rz[R, 1:2], neglam[R, :], ALU.mult), deps=[t2, t_lam])
        t4 = P.op("dve", lambda e: e.tensor_scalar(o1[R, :], O1[:, 0:128], rz[R, 0:1], None, ALU.mult), deps=[t1])
        t5 = P.op("dve", lambda e: e.scalar_tensor_tensor(o[R, :], O2[:, 0:128], rz[R, 2:3], o1[R, :], ALU.mult, ALU.add),
                  deps=[t3, t4])
        res["acc"] = t5
        yield
        t6 = P.op("act", lambda e: e.activation(fin_sq[R, :], o[R, :], AF.Square, accum_out=ss_[R, 0:1]), deps=[t5])
        t7 = P.op("act", lambda e: e.activation(ss_[R, 1:2], ss_[R, 0:1], AF.Ln, bias=1e-5, scale=1.0 / 128), deps=[t6])
        t8 = P.op("act", lambda e: e.activation(ss_[R, 1:2], ss_[R, 1:2], AF.Exp, scale=-0.5), deps=[t7])
        yield
        t9 = P.op("dve", lambda e: e.scalar_tensor_tensor(dst_ap, o[R, :], ss_[R, 1:2], ga_ap, ALU.mult, ALU.mult),
                  deps=[t8] + list(dst_deps))
        fin_state["free"][k] = [t9]
        res["dst"] = t9
        res["tr"] = t9
        yield

    def run_all(gen):
        for _ in gen:
            pass

    t_ags = [None] * 5
    if do_sample:
        ck, cv = io["ck"], io["cv"]
        kc32 = [carve(2048 * i, 2048).rearrange("p (b e) -> p b e", b=16) for i in range(2)]
        vc32 = [carve(4096 + 2048 * i, 2048).rearrange("p (b e) -> p b e", b=16) for i in range(2)]
        kcb = carve(8192, 1024, BF16).rearrange("p (b e) -> p b e", b=16)
        KcT = [carve(9216 + 1040 * i, 1040, BF16) for i in range(2)]
        b2 = Bump(R2)
        Vc = [b2.take([128, 16, 132], BF16) for i in range(3)]
        PTz = b2.take([128, 16, 2, 128], BF16)
        PTn = [b2.take([128, 2, 128], BF16) for i in range(4)]
        aTs2 = b2.take([128, 4, 128], BF16)
        csem = [P.new_dma_sem(f"csem{i}") for i in range(2)]

        t_vc0 = [P.op("pool", lambda e, i=i: e.memset(Vc[i][:, :, 128:132], 0.0)) for i in range(3)]
        t_vc1 = [P.op("pool", lambda e, i=i: e.memset(Vc[i][:, :, 128:129], 1.0), deps=[t_vc0[i]]) for i in range(3)]
        t_z = P.op("pool", lambda e: e.memset(PTz, 0.0))
        t_zn = [P.op("pool", lambda e, i=i: e.memset(PTn[i], 0.0)) for i in range(4)]

        psK = big1[:, 0:512].bitcast(BF16).rearrange("p (b e) -> p b e", b=8)
        S_r = [big0[:, 0:1024], big0[:, 1024:2048]]
        S_new = [big1[:, 1024:1152], big1[:, 1536:1664]]
        acc_s = [big1[:, 512:642], big1[:, 768:898]]

        kc_free = [[], []]
        vc_free = [[], []]
        kcb_free = []
        psK_free = []
        kct_free = [[], []]
        vcb_free = [[], [], []]
        S_free = [[], []]
        Snew_free = []
        bmain = P.cur
        SX = [P.section() for _ in range(16)]
        SY1 = [P.section() for _ in range(16)]
        SY2 = [P.section() for _ in range(16)]
        ptn_free = [[] for _ in range(4)]
        acc_free_s = []
        loads = {}

        def issue_load(s):
            b_ = s % 2
            tk = P.dma("sp", csem[b_], kc32[b_], ck[s], deps=kc_free[b_] + t_A_done)
            tv = P.dma("sp", csem[b_], vc32[b_], cv[s], deps=vc_free[b_])
            loads[s] = (tk, tv)

        issue_load(0)
        t_last = None
        rnd_ctr = 0
        for s in range(16):
            b_ = s % 2
            k = s % 4
            g = s // 4
            tile = 64 + g
            tcol = NPR + 128 * g
            v3 = s % 3
            P.use(SX[s])
            if s + 1 < 16:
                issue_load(s + 1)
            tk, tv = loads[s]
            t_kb = P.op("dve", lambda e, b_=b_: e.tensor_copy(kcb, kc32[b_]), deps=[tk, tv] + kcb_free)
            kc_free[b_] = [t_kb]
            t_vb2 = P.op("act", lambda e, b_=b_, v3=v3: e.copy(Vc[v3][:, :, 0:128], vc32[b_]), deps=[tk, tv, t_vc1[v3]] + vcb_free[v3])
            vc_free[b_] = [t_vb2]
            for rnd in range(2):
                t_tr = None
                for blk in range(8):
                    t_tr = P.op("pe", lambda e, blk=blk, rnd=rnd: e.transpose(psK[:, blk, :], kcb[:, 8 * rnd + blk, :], ident),
                                deps=[t_kb, t_id] + psK_free, signal=(blk == 7))
                t_kt = P.op("act", lambda e, b_=b_, rnd=rnd: e.copy(KcT[b_][:, 1024 * rnd:1024 * rnd + 1024], psK.rearrange("p b e -> p (b e)")),
                            deps=[t_tr] + kct_free[b_])
                psK_free = [t_kt]
            kcb_free = [t_tr]
            P.use(SY1[s])
            t_e = None
            for r in range(4):
                par = rnd_ctr % 2
                rnd_ctr += 1
                Sv = S_r[par].rearrange("p (j b q) -> p j b q", j=2, b=4)
                t_s = None
                for bl in range(4):
                    blk = 4 * r + bl
                    for j in range(2):
                        t_s = P.op("pe", lambda e, bl=bl, blk=blk, j=j, b_=b_, Sv=Sv, tcol=tcol: e.matmul(
                            Sv[:, j, bl, :], lhsT=KcT[b_][64 * j:64 * j + 64, 128 * blk:128 * blk + 128],
                            rhs=QT[64 * j:64 * j + 64, tcol:tcol + 128], start=True, stop=True),
                            deps=[t_kt] + S_free[par] + t_A_done, signal=(bl == 3 and j == 1))
                for j in range(2):
                    t_e = P.op("act", lambda e, Sv=Sv, r=r, j=j, k=k: e.activation(
                        PTz[:, 4 * r:4 * r + 4, j, 32 * k:32 * k + 32], Sv[:, j, :, 32 * k:32 * k + 32], AF.Exp), deps=[t_s, t_z])
                S_free[par] = [t_e]
            t_sn = None
            for j in range(2):
                t_sn = P.op("pe", lambda e, j=j, tcol=tcol: e.matmul(
                    S_new[j], lhsT=KT[64 * j:64 * j + 64, tcol:tcol + 128],
                    rhs=QT[64 * j:64 * j + 64, tcol:tcol + 128], start=True, stop=True),
                    deps=Snew_free + t_A_done, signal=(j == 1))
            t_en = None
            for j in range(2):
                t_en = P.op("act", lambda e, k=k, j=j: e.activation(PTn[k][32 * k:32 * k + 32, j, 32 * k:32 * k + 32],
                                                                   S_new[j][32 * k:32 * k + 32, 32 * k:32 * k + 32], AF.Exp),
                            deps=[t_sn, t_zn[k]] + ptn_free[k])
            Snew_free = [t_en]
            kct_free[b_] = [t_s]
            P.use(SY2[s])
            t_pv = None
            for j in range(2):
                for blk in range(17):
                    first = (k == 0 and j == 0 and blk == 0)
                    lastm = (k == 3 and blk == 16)
                    if blk < 16:
                        fn = lambda e, j=j, blk=blk, v3=v3, first=first, lastm=lastm: e.matmul(
                            acc_s[j], lhsT=PTz[:, blk, j, :], rhs=Vc[v3][:, blk, 0:130],
                            start=first, stop=lastm, skip_group_check=True)
                    else:
                        fn = lambda e, j=j, k=k, tile=tile, first=first, lastm=lastm: e.matmul(
                            acc_s[j], lhsT=PTn[k][:, j, :], rhs=V1[:, tile, 0:130],
                            start=first, stop=lastm, skip_group_check=True)
                    t_pv = P.op("pe", fn, deps=[t_e, t_en, t_vb2] + (acc_free_s if k == 0 else []), signal=(j == 1 and blk == 16))
            vcb_free[v3] = [t_pv]
            ptn_free[k] = [t_pv]
            t_z = P.op("pool", lambda e, k=k: e.memset(PTz.rearrange("p b j q -> p (b j) q")[:, :, 32 * k:32 * k + 32], 0.0), deps=[t_pv])
            if k == 3:
                res = {}
                run_all(finalize(128, acc_s[0], acc_s[1], GA[:, tile, :], aTs2[:, g, :], [t_pv], [], res))
                acc_free_s = [res["acc"]]
                t_last = res["dst"]
        P.use(bmain)
        P.include(SX[0]); P.include(SX[1])
        for s in range(16):
            P.include(SY1[s])
            if s + 2 < 16:
                P.include(SX[s + 2])
            P.include(SY2[s])
        t_exs = P.dma("sp", fsem, exa_dst(16), aTs2, deps=[t_last])
        t_A_done = t_A_done + [t_last, t_pv]
        P.barrier()
        t_ags[4] = io["ag_fn"](4, [t_exs])
    if "pre_sec" in io:
        P.include(io["pre_sec"])

    PT = [carve(512 * i, 512, BF16).rearrange("p (j q) -> p j q", j=2) for i in range(3)]
    aT = [carve(1536 + 256 * i, 256, BF16).rearrange("p (i e) -> p i e", i=4) for i in range(2)]
    Sb = [big0[:, 0:1024].rearrange("p (j q) -> p j q", j=2), big0[:, 1024:2048].rearrange("p (j q) -> p j q", j=2)]

    def acc(j, qi):
        return big1[:, 512 * qi + 256 * j:512 * qi + 256 * j + 130]

    S_free = [[], []]
    PT_free = [[], [], []]
    acc_free = [[] for _ in range(4)]
    tr_free = [[] for _ in range(4)]
    aT_free = [[], []]
    steps = [(t, kb) for t in range(nchunks_attn) for kb in range(4 * t + 4)]

    def emit_qk(i):
        t, kb = steps[i]
        c0 = max(0, 128 * (kb - 4 * t))
        Sv = Sb[i % 2]
        t_s = None
        for j in range(2):
            t_s = P.op("pe", lambda e, j=j, kb=kb, c0=c0, Sv=Sv, t=t: e.matmul(
                Sv[:, j, c0:CH], lhsT=KT[64 * j:64 * j + 64, 128 * kb:128 * kb + 128],
                rhs=QT[64 * j:64 * j + 64, CH * t + c0:CH * (t + 1)], start=True, stop=True),
                deps=S_free[i % 2] + t_A_done, signal=(j == 1))
        return t_s

    t_ad_last = [None, None]
    DA, DT = 8, 10
    fb = Bump(R2)
    fb.off = 10600
    fo = [[fb.take([128, 128], F32) for _ in range(4)] for _ in range(2)]
    fo1 = fb.take([128, 128], F32)
    fsq = fb.take([128, 128], F32)
    frz = fb.take([128, 2, 4, 4], F32)
    fss = fb.take([128, 2, 4], F32)
    fln = fb.take([128, 2, 4], F32)
    frs = fb.take([128, 2, 4], F32)
    fr_free = [[], []]
    ta_tok = [[], []]
    last_dve = {"t5": [], "t7": []}
    pendA = []
    pendT = []

    def emit_act_stage(tt, toks):
        par = tt % 2
        tA = P.op("act", lambda e, par=par: e.activation(fln[:, par, :], fss[:, par, :], AF.Ln, bias=1e-5, scale=1.0 / 128),
                  deps=toks + fr_free[par])
        ta_tok[par] = [tA]
        return P.op("act", lambda e, par=par: e.activation(frs[:, par, :], fln[:, par, :], AF.Exp, scale=-0.5), deps=[tA])

    def emit_tail(tt, tB):
        par = tt % 2
        t9 = None
        for qi in range(4):
            t9 = P.op("dve", lambda e, par=par, qi=qi, tt=tt: e.scalar_tensor_tensor(
                aT[par][:, qi, :], fo[par][qi], frs[:, par, qi:qi + 1], GA[:, 4 * tt + qi, :], ALU.mult, ALU.mult),
                deps=[tB] + (aT_free[par] if qi == 0 else []))
        fr_free[par] = [t9]
        t_ad = P.dma("sp", asem[par], exa_dst(tt), aT[par], deps=[t9])
        aT_free[par] = [t_ad]
        t_ad_last[par] = t_ad
        if tt % 4 == 3:
            prev = [x for x in t_ags if x is not None]
            t_ags[tt // 4] = io["ag_fn"](tt // 4, [t_ad_last[0], t_ad_last[1]] + prev)
            if tt == 7 and "pre2_sec" in io:
                P.include(io["pre2_sec"])

    t_s_next = emit_qk(0) if steps else None
    for i, (t, kb) in enumerate(steps):
        r = kb - 4 * t
        c0 = max(0, 128 * r)
        Sv = Sb[i % 2]
        ptb = PT[i % 3]
        t_s = t_s_next
        if i + 1 < len(steps):
            t_s_next = emit_qk(i + 1)
        t_e = P.op("act", lambda e, ptb=ptb, Sv=Sv, c0=c0: e.activation(ptb[:, :, c0:CH], Sv[:, :, c0:CH], AF.Exp),
                   deps=[t_s] + PT_free[i % 3])
        S_free[i % 2] = [t_e]
        for rec in [x for x in pendA if i >= x[1] + DA]:
            pendA.remove(rec)
            pendT.append([rec[0], rec[1], emit_act_stage(rec[0], rec[2])])
        t_m = t_e
        if r >= 0:
            t_m = P.op("pool", lambda e, ptb=ptb, c0=c0: e.memset(ptb[64:128, :, c0:c0 + 64], 0.0), deps=[t_e])
        t_pv = None
        qis = list(range(max(r, 0), 4))
        for qi in qis:
            for j in range(2):
                last = (qi == qis[-1] and j == 1)
                t_pv = P.op("pe", lambda e, j=j, qi=qi, kb=kb, ptb=ptb, t=t: e.matmul(
                    acc(j, qi), lhsT=ptb[:, j, 128 * qi:128 * qi + 128], rhs=V1[:, kb, 0:130],
                    start=(kb == 0 and j == 0), stop=(kb == 4 * t + qi), skip_group_check=True),
                    deps=[t_m] + (acc_free[qi] if kb == 0 else []), signal=last)
        PT_free[i % 3] = [t_pv]
        for rec in [x for x in pendT if i >= x[1] + DT]:
            pendT.remove(rec)
            emit_tail(rec[0], rec[2])
        if r >= 0:
            qi = r
            par = t % 2
            O1, O2 = acc(0, qi), acc(1, qi)
            rz = frz[:, par, qi, :]
            o = fo[par][qi]
            t1 = P.op("dve", lambda e, rz=rz, O1=O1: e.reciprocal(rz[:, 0:1], O1[:, 128:129]), deps=[t_pv] + fr_free[par])
            t2 = P.op("dve", lambda e, rz=rz, O2=O2: e.reciprocal(rz[:, 1:2], O2[:, 128:129]), deps=[t_pv] + fr_free[par])
            t3 = P.op("dve", lambda e, rz=rz: e.tensor_tensor(rz[:, 2:3], rz[:, 1:2], neglam, ALU.mult), deps=[t2, t_lam])
            t4 = P.op("dve", lambda e, rz=rz, O1=O1: e.tensor_scalar(fo1, O1[:, 0:128], rz[:, 0:1], None, ALU.mult),
                      deps=[t1] + last_dve["t5"])
            t5 = P.op("dve", lambda e, rz=rz, O2=O2, o=o: e.scalar_tensor_tensor(o, O2[:, 0:128], rz[:, 2:3], fo1, ALU.mult, ALU.add),
                      deps=[t3, t4] + fr_free[par])
            last_dve["t5"] = [t5]
            acc_free[qi] = [t5]
            if qi == 3:
                t7 = None
                for q2 in range(4):
                    t6 = P.op("dve", lambda e, par=par, q2=q2: e.tensor_tensor(fsq, fo[par][q2], fo[par][q2], ALU.mult),
                              deps=[t5] + last_dve["t7"])
                    t7 = P.op("dve", lambda e, par=par, q2=q2: e.tensor_reduce(fss[:, par, q2:q2 + 1], fsq, mybir.AxisListType.X, ALU.add),
                              deps=[t6] + ta_tok[par])
                    last_dve["t7"] = [t7]
                pendA.append([t, i, [t7]])
    for rec in pendA:
        pendT.append([rec[0], rec[1], emit_act_stage(rec[0], rec[2])])
    for rec in pendT:
        emit_tail(rec[0], rec[2])
    return {"R2": R2, "t_ags": t_ags, "QT": QT, "KT": KT, "V1": V1, "GA": GA, "carve": carve, "big0": big0, "big1": big1, "ident": ident, "t_id": t_id}


def emit_phase_c2(nc, P, io, Gs, t_ags, loc2s, G2s, H):
    carve, big0, big1, ident, t_id = H["carve"], H["big0"], H["big1"], H["ident"], H["t_id"]
    Gv_a = [g_.rearrange("(r h p n) e -> p r h n e", r=4, h=2, p=128) for g_ in Gs]
    Gv_p = [g_.rearrange("(r h c n) e -> c r h n e", r=4, h=2, c=128) for g_ in Gs]
    c2b = Bump(H["R2"])
    NLB = 3
    Aload = [c2b.take([128, 4, 4, 128], BF16) for i in range(NLB)]
    ATp = [c2b.take([128, 4, 4, 128], BF16) for i in range(NLB)]
    ATa = [carve(4096 + 1024 * i, 1024, BF16).rearrange("p (r n e) -> p r n e", r=4, n=4) for i in range(2)]
    xr = [c2b.take([128, 4, 256], F32) for i in range(NLB)]
    wo32 = carve(8192, 2048).rearrange("p (k d) -> p k d", k=8)
    Wo = carve(10240, 1024, BF16).rearrange("p (k d) -> p k d", k=8)
    fgj = carve(11264, 256)
    junk = carve(11520, 256)
    ss = carve(11776, 68)
    gs = carve(11844, 272).rearrange("p (r c) -> p r c", r=4)
    rstd = carve(12116, 68)
    lnt = carve(12184, 68)
    yk = []
    yreg = []
    for nm in ("QT", "KT", "V1", "GA"):
        t_ = H[nm]
        flat = t_ if len(t_.shape) == 2 else t_.rearrange("p n e -> p (n e)")
        f32v = flat.bitcast(F32)
        yreg.append(f32v)
        for i in range(17):
            yk.append(f32v[:, 256 * i:256 * i + 256])
    yv4 = io["y_out"].rearrange("(p m) d -> p m d", p=128)
    pso = [big0[:, 0:1024].rearrange("p (n d) -> p n d", n=4), big0[:, 1024:2048].rearrange("p (n d) -> p n d", n=4)]
    psA = [big1[:, 1024 * i:1024 * i + 1024].bitcast(BF16).rearrange("p (r n e) -> p r n e", r=4, n=4) for i in range(2)]

    wsem = P.new_dma_sem("c_wsem")
    gl = [P.new_dma_sem(f"c_gl{i}") for i in range(NLB)]
    xsm = [P.new_dma_sem(f"c_xs{i}") for i in range(NLB)]
    ysem = P.new_dma_sem("c_ysem")
    cur_ = P.cur
    if "pre_sec" in io:
        P.use(io["pre_sec"])
    P.dma("sp", wsem, wo32, io["w_out_j"].rearrange("(k p) n -> p k n", p=128))
    t_w = P.dma("sp", wsem, fgj, io["fgj"])
    P.use(cur_)
    t_wo = P.op("dve", lambda e: e.tensor_copy(Wo, wo32), deps=[t_w])
    xv = io["xres"].rearrange("(p m) d -> p m d", p=128)
    yv = io["y_out"].rearrange("(p m) d -> m p d", p=128)
    al_free = [[] for _ in range(NLB)]
    atp_free = [[] for _ in range(NLB)]
    ata_free = [[], []]
    xr_free = [[] for _ in range(NLB)]
    psA_free = [[], []]
    pso_free = [[], []]
    t_sq = None
    ssem = P.new_dma_sem("c_ssem")
    cc_prev = []
    cmain = P.cur
    Nsec = [P.section() for _ in range(2)]
    for t in range(NCH):
        b_ = t % 2
        l_ = t % NLB
        q_ = min(t // 4, 4)
        lt = t % 4 if t < 16 else 0
        t_ag = t_ags[q_]
        sec_ = P.cur
        if t < NLB and "pre2_sec" in io:
            P.use(io["pre2_sec"])
        for r in range(4):
            t_la = P.dma("sp", gl[l_], Aload[l_][:, r], Gv_a[q_][:, r, 0, 4 * lt:4 * lt + 4, :], deps=[t_ag] + al_free[l_])
        for r in range(4):
            t_lp = P.dma("sp", gl[l_], ATp[l_][:, r], Gv_p[q_][:, r, 1, 4 * lt:4 * lt + 4, :], deps=[t_ag] + atp_free[l_])
        t_lx = P.dma("sp", xsm[l_], xr[l_], xv[:, 4 * t:4 * t + 4, :], deps=xr_free[l_])
        P.use(sec_)
        t_tr = None
        for r in range(4):
            for n in range(4):
                t_tr = P.op("pe", lambda e, r=r, n=n, b_=b_, l_=l_: e.transpose(psA[b_][:, r, n, :], Aload[l_][:, r, n, :], ident),
                            deps=[t_la, t_lp, t_id] + psA_free[b_], signal=(r == 3 and n == 3))
        al_free[l_] = [t_tr]
        t_cp = P.op("act", lambda e, b_=b_: e.copy(ATa[b_].rearrange("p r n e -> p (r n e)"), psA[b_].rearrange("p r n e -> p (r n e)")),
                    deps=[t_tr] + ata_free[b_])
        psA_free[b_] = [t_cp]
        t_mm = None
        for n in range(4):
            for k in range(8):
                src = ATa[b_] if k < 4 else ATp[l_]
                t_mm = P.op("pe", lambda e, n=n, k=k, b_=b_, src=src: e.matmul(
                    pso[b_][:, n, :], lhsT=src[:, k % 4, n, :], rhs=Wo[:, k, :], start=(k == 0), stop=(k == 7)),
                    deps=[t_cp, t_lp, t_wo] + pso_free[b_], signal=(n == 3 and k == 7))
        ata_free[b_] = [t_mm]
        atp_free[l_] = [t_mm]
        t_y = None
        for n in range(4):
            yk_ = yk[4 * t + n]
            t_y = P.op("dve", lambda e, n=n, b_=b_, l_=l_, yk_=yk_: e.tensor_tensor(yk_, pso[b_][:, n, :], xr[l_][:, n, :], ALU.add),
                       deps=[t_mm, t_lx])
            t_sq = P.op("act", lambda e, yk_=yk_, i_=4 * t + n: e.activation(junk, yk_, AF.Square, accum_out=ss[:, i_:i_ + 1]),
                        deps=[t_y] + ([t_sq] if t_sq is not None else []))
        pso_free[b_] = [t_y]
        xr_free[l_] = [t_y]
        if t == 11 or t == NCH - 1:
            q_n = 0 if t == 11 else 1
            lo, hi = (0, 48) if q_n == 0 else (48, NTILE)
            t_sd = P.dma("sp", ssem, loc2s[q_n], ss[:, lo:hi], deps=[t_sq])
            t_ag2 = P.coll(lambda e, q_n=q_n: e.collective_compute("AllGather", ALU.bypass, replica_groups=[[0, 1, 2, 3], [4, 5, 6, 7]],
                                                                  ins=[loc2s[q_n].opt()], outs=[G2s[q_n].opt()]),
                           deps=[t_sd] + [x for x in t_ags if x is not None] + cc_prev)
            P.use(Nsec[q_n])
            cc_prev = [t_ag2]
            t_g = P.dma("sp", ssem, gs[:, :, lo:hi], G2s[q_n].rearrange("(r p) c -> p r c", p=128), deps=[t_ag2, t_sd])
            t_a = P.op("dve", lambda e, lo=lo, hi=hi: e.tensor_tensor(rstd[:, lo:hi], gs[:, 0, lo:hi], gs[:, 1, lo:hi], ALU.add), deps=[t_g])
            t_a = P.op("dve", lambda e, lo=lo, hi=hi: e.tensor_tensor(rstd[:, lo:hi], rstd[:, lo:hi], gs[:, 2, lo:hi], ALU.add), deps=[t_a])
            t_a = P.op("dve", lambda e, lo=lo, hi=hi: e.tensor_tensor(rstd[:, lo:hi], rstd[:, lo:hi], gs[:, 3, lo:hi], ALU.add), deps=[t_a])
            t_a = P.op("act", lambda e, lo=lo, hi=hi: e.activation(lnt[:, lo:hi], rstd[:, lo:hi], AF.Ln, bias=1e-6, scale=1.0 / D), deps=[t_a])
            t_a = P.op("act", lambda e, lo=lo, hi=hi: e.activation(rstd[:, lo:hi], lnt[:, lo:hi], AF.Exp, scale=-0.5), deps=[t_a])
            i = lo
            while i < hi:
                reg, j0 = i // 17, i % 17
                n_ = min(4, 17 - j0, hi - i)
                t_o = None
                for k_ in range(n_):
                    t_o = P.op("dve", lambda e, i_=i + k_: e.scalar_tensor_tensor(yk[i_], yk[i_], rstd[:, i_:i_ + 1], fgj, ALU.mult, ALU.mult),
                               deps=[t_a, t_w])
                P.dma("sp", ysem, yv4[:, i:i + n_, :], yreg[reg][:, 256 * j0:256 * (j0 + n_)].rearrange("p (n d) -> p n d", n=n_), deps=[t_o])
                i += n_
            P.use(cmain)
    P.use(cmain)
    P.include(Nsec[0])
    P.include(Nsec[1])
    return ysem


def build_fused(nchunks_attn=16, do_sample=True):
    nc = bass.Bass("TRN2", target_bir_lowering=False)
    io = {}

    def inp(name, shape, dt=F32):
        io[name] = nc.dram_tensor(name, shape, dt, kind="ExternalInput").ap()

    def outp(name, shape, dt=F32):
        io[name] = nc.dram_tensor(name, shape, dt, kind="ExternalOutput").ap()

    inp("xT", [D, NT]); inp("w_tm", [D, 512]); inp("w_fm", [D, 256])
    inp("gcol", [128, 8]); inp("subln", [128, 128]); inp("pscale", [128, 1]); inp("meta", [128, 32])
    inp("lamv", [128, 4, 64]); inp("wpool", [128, 128]); inp("cc", [NT, 16]); inp("ss", [NT, 16])
    inp("spT", [128, 16, 15]); inp("ck", [16, 128, 16, 128]); inp("cv", [16, 128, 16, 128])
    inp("w_out_j", [D, 256]); inp("fgj", [128, 256]); inp("xres", [NT, 256])
    outp("k_out", [NT, 128]); outp("v_out", [NT, 128]); outp("pool_out", [17, 15, 128])
    outp("y_out", [NT, 256])
    locs = [nc.dram_tensor(f"xloc{q}", [4096, 128], BF16).ap() for q in range(4)] + [nc.dram_tensor("xloc4", [1024, 128], BF16).ap()]
    Gs = [nc.dram_tensor(f"xg{q}", [4 * 4096, 128], BF16).ap() for q in range(4)] + [nc.dram_tensor("xg4", [4 * 1024, 128], BF16).ap()]
    loc2 = [nc.dram_tensor(f"xss_loc{q}", [128, 48 if q == 0 else 20], F32).ap() for q in range(2)]
    G2 = [nc.dram_tensor(f"xss_g{q}", [4 * 128, 48 if q == 0 else 20], F32).ap() for q in range(2)]

    def exa_dst(t):
        if t < 16:
            return locs[t // 4][0:2048, :].rearrange("(p n) e -> p n e", p=128)[:, 4 * (t % 4):4 * (t % 4) + 4, :]
        return locs[4][0:512, :].rearrange("(p n) e -> p n e", p=128)

    def exp_dst(t):
        if t < 16:
            return locs[t // 4][2048:4096, :].rearrange("(c n) e -> c (n e)", c=128)[:, 512 * (t % 4):512 * (t % 4) + 512]
        return locs[4][512:1024, :].rearrange("(c n) e -> c (n e)", c=128)

    io["exa_dst"] = exa_dst
    io["exp_dst"] = exp_dst
    P = Prog(nc)
    GR = [[0, 1, 2, 3], [4, 5, 6, 7]]

    def ag_fn(q, deps):
        return P.coll(lambda e, q=q: e.collective_compute("AllGather", ALU.bypass, replica_groups=GR,
                                                          ins=[locs[q].opt()], outs=[Gs[q].opt()]), deps=deps)

    io["ag_fn"] = ag_fn
    io["pre_sec"] = P.section()
    io["pre2_sec"] = P.section()
    H = emit_phase_ab(nc, P, io, nchunks_attn, do_sample)
    P.barrier()
    t_ags = H["t_ags"]
    emit_phase_c2(nc, P, io, Gs, t_ags, loc2, G2, H)
    P.wait_all_dma("sp")
    P.finish()
    return nc


def _rope_tables():
    inv = (np.float32(500000.0) ** (-np.arange(0, 16, 2, dtype=np.float32) / np.float32(16))).astype(np.float32)
    pos = np.concatenate([np.arange(NPR), np.tile(PAST + np.arange(32), 16)]).astype(np.float32)
    ang = (pos[:, None] * inv[None, :]).astype(np.float32)
    c = np.cos(ang).astype(np.float32)
    s = np.sin(ang).astype(np.float32)
    return np.ascontiguousarray(np.concatenate([c, c], 1)), np.ascontiguousarray(np.concatenate([-s, s], 1))


def prep_ab(inp):
    cc, ss = _rope_tables()
    maps = []
    w_in = inp["w_in"][0]
    for c in range(8):
        b, h = divmod(c, 4)
        xs = inp["x_sample"][16 * b:16 * b + 16].reshape(NSM, D)
        xT = np.ascontiguousarray(np.concatenate([inp["x_prompt"][b], xs], 0).T)
        cols = lambda base: w_in[:, base + 128 * h: base + 128 * h + 128]
        w_tm = np.ascontiguousarray(np.concatenate([cols(0), cols(512), cols(1024), cols(1536)], 1))
        w_fm = np.ascontiguousarray(np.concatenate([cols(2048), cols(2560)], 1))
        w = POOL_WINDOWS[h]
        meta = np.zeros((128, 32), np.float32)
        meta[:, h] = 1.0
        meta[:, 4] = 1.0 / w
        meta[:, 8:24] = (w / np.minimum(np.arange(16) + 1, w)).astype(np.float32)[None, :]
        lamv = np.stack([inp["lambda_q1"][0], inp["lambda_k1"][0], inp["lambda_q2"][0], inp["lambda_k2"][0]], 0)
        maps.append({
            "xT": xT, "w_tm": w_tm, "w_fm": w_fm,
            "gcol": np.ascontiguousarray(inp["norm_g"][0].reshape(8, 128).T),
            "subln": np.ascontiguousarray(np.broadcast_to(inp["subln_g"][0][None, :], (128, 128))),
            "pscale": np.ascontiguousarray(inp["pool_scale"][0][128 * h:128 * h + 128].reshape(128, 1)),
            "meta": meta,
            "lamv": np.ascontiguousarray(np.broadcast_to(lamv[None], (128, 4, 64))),
            "wpool": np.ascontiguousarray(inp["w_pool"][0, h]),
            "cc": cc, "ss": ss,
            "spT": np.ascontiguousarray(inp["state_pool"][0, 16 * b:16 * b + 16, :, 128 * h:128 * h + 128].transpose(2, 0, 1)),
            "ck": np.ascontiguousarray(inp["cache_k"][0, 16 * b:16 * b + 16, :, h, :].reshape(16, 16, 128, 128).transpose(0, 2, 1, 3)),
            "cv": np.ascontiguousarray(inp["cache_v"][0, 16 * b:16 * b + 16, :, h, :].reshape(16, 16, 128, 128).transpose(0, 2, 1, 3)),
        })
    return maps


def kernel(**inputs):
    inp = {k: np.asarray(v) for k, v in inputs.items()}
    maps = prep_ab(inp)
    for c in range(8):
        b, j = divmod(c, 4)
        xs = inp["x_sample"][16 * b:16 * b + 16].reshape(NSM, D)
        maps[c]["w_out_j"] = np.ascontiguousarray(inp["w_out"][0][:, 256 * j:256 * j + 256])
        maps[c]["fgj"] = np.ascontiguousarray(np.broadcast_to(inp["final_g"][None, 256 * j:256 * j + 256], (128, 256)))
        xr_ = np.concatenate([inp["x_prompt"][b][:, 256 * j:256 * j + 256], xs[:, 256 * j:256 * j + 256]], 0)
        maps[c]["xres"] = np.ascontiguousarray(xr_.reshape(NTILE, 128, 256).transpose(1, 0, 2)).reshape(NT, 256)
    nc = build_fused()
    res = run_bass_kernel_spmd(nc, maps, core_ids=list(range(8))).results
    return assemble(res, res)


def assemble(res1, res2):
    y_prompt = np.empty((2, NPR, D), np.float32)
    y_sample = np.empty((32, 32, D), np.float32)
    k_prompt = np.empty((1, 2, NPR, 4, 128), np.float32)
    v_prompt = np.empty((1, 2, NPR, 4, 128), np.float32)
    pool_prompt = np.empty((1, 2, 15, 512), np.float32)
    k_sample = np.empty((1, 32, 32, 4, 128), np.float32)
    v_sample = np.empty((1, 32, 32, 4, 128), np.float32)
    pool_sample = np.empty((1, 32, 15, 512), np.float32)
    for c in range(8):
        b, h = divmod(c, 4)
        r = dict(res1[c])
        for nm_, w_ in (("k_out", 128), ("v_out", 128), ("y_out", 256)):
            if nm_ in r:
                r[nm_] = np.asarray(r[nm_]).reshape(128, NTILE, w_).transpose(1, 0, 2).reshape(NT, w_)
        k_prompt[0, b, :, h, :] = r["k_out"][:NPR]
        v_prompt[0, b, :, h, :] = r["v_out"][:NPR]
        k_sample[0, 16 * b:16 * b + 16, :, h, :] = r["k_out"][NPR:].reshape(16, 32, 128)
        v_sample[0, 16 * b:16 * b + 16, :, h, :] = r["v_out"][NPR:].reshape(16, 32, 128)
        pool_prompt[0, b, :, 128 * h:128 * h + 128] = r["pool_out"][16]
        pool_sample[0, 16 * b:16 * b + 16, :, 128 * h:128 * h + 128] = r["pool_out"][:16]
        if res2 is not None:
            y = r["y_out"]
            y_prompt[b, :, 256 * h:256 * h + 256] = y[:NPR]
            y_sample[16 * b:16 * b + 16, :, 256 * h:256 * h + 256] = y[NPR:].reshape(16, 32, 256)
    return (y_prompt, y_sample, k_prompt, v_prompt, pool_prompt, k_sample, v_sample, pool_sample)
```

```python
import types
import numpy as np
import ml_dtypes
import concourse.bass as bass
import concourse.mybir as mybir
from concourse.bass_utils import run_bass_kernel_spmd

F32 = mybir.dt.float32
BF16 = mybir.dt.bfloat16
I32 = mybir.dt.int32
AF = mybir.ActivationFunctionType
ALU = mybir.AluOpType

D = 1024
NPR = 8192
NSM = 512
NT = NPR + NSM
CH = 512
NCH = NT // CH
NTILE = NT // 128
NQ = 2176
PAST = 2048
LAM_INIT = 0.2
POOL_WINDOWS = (2, 4, 8, 16)


def _freeze(fn):
    if fn.__closure__ is None:
        return fn
    cells = []
    for c in fn.__closure__:
        try:
            cells.append(types.CellType(c.cell_contents))
        except ValueError:
            cells.append(c)
    return types.FunctionType(fn.__code__, fn.__globals__, fn.__name__, fn.__defaults__, tuple(cells))


class Tok:
    __slots__ = ("kind", "eng", "h", "val")

    def __init__(self, kind, eng=None):
        self.kind, self.eng, self.h, self.val = kind, eng, None, None


class Prog:
    ENGS = ("pe", "act", "dve", "pool", "sp")

    def __init__(self, nc):
        self.nc = nc
        self.q = {e: [] for e in self.ENGS}
        self.sem = {e: nc.alloc_semaphore(name=f"s_{e}") for e in self.ENGS if e != "sp"}
        self.tick = {e: 0 for e in self.ENGS}
        self.waited = {e: {} for e in self.ENGS}
        self.dma_sems = []
        self.main = []
        self.cur = self.main
        self.ncoll = 0

    def section(self):
        return []

    def use(self, sec):
        self.cur = sec

    def include(self, sec):
        self.cur.append(("sec", sec))

    def op(self, eng, fn, deps=(), signal=True):
        tok = Tok("eng", eng) if signal else None
        self.cur.append(("op", eng, _freeze(fn), list(deps), tok))
        return tok

    def new_dma_sem(self, name):
        h = self.nc.alloc_semaphore(name=name)
        st = {"h": h, "v": 0}
        self.dma_sems.append(st)
        return st

    def dma(self, queue, st, out, in_, deps=(), **kw):
        tok = Tok("dma")
        self.cur.append(("dma", queue, st, out, in_, list(deps), kw, tok))
        return tok

    def coll(self, fn, deps=()):
        tok = Tok("dma")
        self.cur.append(("coll", _freeze(fn), list(deps), tok))
        return tok

    def wait_only(self, eng, deps):
        self.cur.append(("wait", eng, list(deps)))

    def barrier(self):
        self.cur.append(("barrier",))

    def wait_all_dma(self, eng):
        self.cur.append(("waitall", eng))

    def _emit_waits(self, eng, deps):
        out = []
        best = {}
        for d in deps:
            if d is None:
                continue
            if isinstance(d, Tok):
                assert d.val is not None, "dependency flushed after its consumer"
                k_ = ("e", d.eng) if d.kind == "eng" else ("d", id(d.h))
                v_ = d.val
            else:
                k_ = ("d", id(d[1])); v_ = d[2]
            if k_ not in best or v_ > best[k_][0]:
                best[k_] = (v_, d)
        for _, d in best.values():
            if isinstance(d, Tok):
                assert d.val is not None, "dependency flushed after its consumer"
                if d.kind == "eng":
                    if d.eng == eng and eng == "pe":
                        continue
                    key = ("eng", d.eng); h = self.sem[d.eng]
                else:
                    key = ("dma", id(d.h)); h = d.h
                v = d.val
            else:
                key = ("dma", id(d[1])); h = d[1]; v = d[2]
            if self.waited[eng].get(key, 0) >= v:
                continue
            self.waited[eng][key] = v
            out.append((h, v))
        return out

    def _all_toks(self):
        toks = []
        for e in ("pe", "act", "dve", "pool"):
            if self.tick[e] > 0:
                t = Tok("eng", e); t.val = self.tick[e]
                toks.append(t)
        toks += [("dma", st["h"], st["v"]) for st in self.dma_sems if st["v"] > 0]
        return toks

    def _push_wait(self, eng, deps):
        waits = self._emit_waits(eng, deps)
        if waits:
            def run(e, waits=waits):
                for h, v in waits:
                    e.wait_ge(h, v)
            self.q[eng].append(run)

    def _flush(self, sec):
        for rec in sec:
            kind = rec[0]
            if kind == "sec":
                self._flush(rec[1])
            elif kind == "op":
                _, eng, fn, deps, tok = rec
                waits = self._emit_waits(eng, deps)
                sem = self.sem[eng]
                if tok is not None:
                    self.tick[eng] += 1
                    tok.val = self.tick[eng]

                def run(e, waits=waits, fn=fn, sig=(tok is not None), sem=sem):
                    for h, v in waits:
                        e.wait_ge(h, v)
                    ins = fn(e)
                    if sig:
                        ins.then_inc(sem, 1)
                self.q[eng].append(run)
            elif kind == "dma":
                _, queue, st, out, in_, deps, kw, tok = rec
                waits = self._emit_waits(queue, deps)
                st["v"] += 16
                tok.h, tok.val = st["h"], st["v"]

                def run(e, waits=waits, out=out, in_=in_, kw=kw, h=st["h"]):
                    for hh, v in waits:
                        e.wait_ge(hh, v)
                    e.dma_start(out=out, in_=in_, **kw).then_inc(h, 16)
                self.q[queue].append(run)
            elif kind == "coll":
                _, fn, deps, tok = rec
                waits = self._emit_waits("pool", deps)
                h = self.nc.alloc_semaphore(name=f"cc_sem{self.ncoll}")
                self.ncoll += 1
                tok.h, tok.val = h, 1

                def run(e, waits=waits, fn=fn, h=h):
                    for hh, v in waits:
                        e.wait_ge(hh, v)
                    fn(e).then_inc(h)
                self.q["pool"].append(run)
            elif kind == "wait":
                self._push_wait(rec[1], rec[2])
            elif kind == "barrier":
                toks = self._all_toks()
                for e in self.ENGS:
                    self._push_wait(e, toks)
            elif kind == "waitall":
                self._push_wait(rec[1], [("dma", st["h"], st["v"]) for st in self.dma_sems if st["v"] > 0])

    def finish(self):
        nc = self.nc
        self._flush(self.main)
        with nc.Block() as block:
            @block.tensor
            def _(e):
                for f in self.q["pe"]:
                    f(e)

            @block.scalar
            def _(e):
                for f in self.q["act"]:
                    f(e)

            @block.vector
            def _(e):
                for f in self.q["dve"]:
                    f(e)

            @block.gpsimd
            def _(e):
                for f in self.q["pool"]:
                    f(e)

            @block.sync
            def _(e):
                for f in self.q["sp"]:
                    f(e)


class FinG:
    def __init__(self, gen, res, t, qi):
        self.gen, self.res, self.t, self.qi, self.age = gen, res, t, qi, 0

    def __next__(self):
        return next(self.gen)


class Ring:
    def __init__(self, bufs):
        self.bufs = bufs
        self.free = [[] for _ in bufs]
        self.i = -1

    def next(self):
        self.i = (self.i + 1) % len(self.bufs)
        deps = self.free[self.i]
        self.free[self.i] = []
        return self.bufs[self.i], deps, self.i

    def release(self, idx, *toks):
        self.free[idx].extend(t for t in toks if t is not None)


class Bump:
    def __init__(self, slab):
        self.slab, self.off, self.n = slab, 0, slab.shape[1]

    def take(self, shape, dt):
        assert shape[0] == 128
        cnt = 1
        for d in shape[1:]:
            cnt *= d
        nw = cnt if dt == F32 else (cnt + 1) // 2
        nw = (nw + 7) // 8 * 8
        assert self.off + nw <= self.n, ("slab overflow", self.off, nw, self.n)
        v = self.slab[:, self.off:self.off + nw]
        self.off += nw
        if dt != F32:
            v = v.bitcast(dt)
        v = v[:, 0:cnt]
        if len(shape) == 2:
            return v
        names = " ".join(f"d{i}" for i in range(len(shape) - 1))
        kw = {f"d{i}": shape[i + 1] for i in range(len(shape) - 2)}
        return v.rearrange(f"p ({names}) -> p {names}", **kw)


def sb(nc, name, shape, dt):
    return nc.alloc_sbuf_tensor("sb_" + name, shape, dt).ap()


def emit_phase_ab(nc, P, io, nchunks_attn=16, do_sample=True):
    xT, w_tm, w_fm = io["xT"], io["w_tm"], io["w_fm"]

    big0 = nc.alloc_psum_tensor("big0", [128, 2048], F32).ap()
    big1 = nc.alloc_psum_tensor("big1", [128, 2048], F32).ap()

    def bank(k):
        bg = big0 if k < 4 else big1
        return bg[:, 512 * (k % 4):512 * (k % 4) + 512]

    QT = sb(nc, "QT", [128, NT], BF16)
    KT = sb(nc, "KT", [128, NT], BF16)
    V1 = sb(nc, "V1", [128, NTILE, 132], BF16)
    GA = sb(nc, "GA", [128, NTILE, 128], BF16)
    ident = sb(nc, "ident", [128, 128], BF16)
    identf = sb(nc, "identf", [128, 128], F32)
    ones_bf = sb(nc, "ones_bf", [128, 128], BF16)
    one_f = sb(nc, "one_f", [1, 1], F32)
    Wtm = sb(nc, "Wtm", [128, 8, 512], BF16)
    Wfm = sb(nc, "Wfm", [128, 8, 256], BF16)
    Wp = sb(nc, "Wp", [128, 128], BF16)
    wp32 = sb(nc, "wp32", [128, 128], F32)
    gcol = sb(nc, "gcol", [128, 8], F32)
    subln = sb(nc, "subln", [128, 128], F32)
    pscale = sb(nc, "pscale", [128, 1], F32)
    meta = sb(nc, "meta", [128, 32], F32)
    lamv = sb(nc, "lamv", [128, 4, 64], F32)
    lamt = sb(nc, "lamt", [128, 2, 64], F32)
    lams = sb(nc, "lams", [128, 2], F32)
    neglam = sb(nc, "neglam", [128, 1], F32)
    CC = sb(nc, "CC", [128, NTILE, 16], F32)
    SS = sb(nc, "SS", [128, NTILE, 16], F32)

    cst = P.new_dma_sem("cst")
    t_c = []
    t_c.append(P.dma("sp", cst, gcol, io["gcol"]))
    t_c.append(P.dma("sp", cst, subln, io["subln"]))
    t_c.append(P.dma("sp", cst, pscale, io["pscale"]))
    t_c.append(P.dma("sp", cst, meta, io["meta"]))
    t_c.append(P.dma("sp", cst, lamv, io["lamv"]))
    t_c.append(P.dma("sp", cst, wp32, io["wpool"]))
    t_c.append(P.dma("sp", cst, CC, io["cc"].rearrange("(n p) e -> p n e", p=128)))
    t_c.append(P.dma("sp", cst, SS, io["ss"].rearrange("(n p) e -> p n e", p=128)))
    t_cst = t_c[-1]

    t_idf = P.op("pool", lambda e: e.memset(identf, 0.0))
    t_idf = P.op("pool", lambda e: e.affine_select(identf, identf, [[-1, 128]], ALU.not_equal, 1.0,
                                                   base=0, channel_multiplier=1), deps=[t_idf])
    t_id = P.op("dve", lambda e: e.tensor_copy(ident, identf), deps=[t_idf])
    t_ones = P.op("pool", lambda e: e.memset(ones_bf, 1.0))
    t_onef = P.op("pool", lambda e: e.memset(one_f, 1.0))
    t_v1 = P.op("pool", lambda e: e.memset(V1[:, :, 128:132], 0.0))
    t_v1 = P.op("pool", lambda e: e.memset(V1[:, :, 128:129], 1.0), deps=[t_v1])
    t_wp = P.op("dve", lambda e: e.tensor_copy(Wp, wp32), deps=[t_cst])
    t_sub = P.op("dve", lambda e: e.tensor_scalar(subln, subln, 1.0 - LAM_INIT, None, ALU.mult), deps=[t_cst])
    t_l = P.op("dve", lambda e: e.tensor_tensor(lamt, lamv[:, 0:4:2, :], lamv[:, 1:4:2, :], ALU.mult), deps=[t_cst])
    t_l = P.op("dve", lambda e: e.tensor_reduce(lams, lamt, mybir.AxisListType.X, ALU.add), deps=[t_l])
    t_l = P.op("act", lambda e: e.activation(lams, lams, AF.Exp), deps=[t_l])
    t_lam = P.op("dve", lambda e: e.scalar_tensor_tensor(neglam, lams[:, 1:2], -LAM_INIT, lams[:, 0:1],
                                                        ALU.add, ALU.subtract), deps=[t_l])

    R32 = sb(nc, "R32", [128, 12288], F32)

    def carve(off, n, dt=F32):
        v = R32[:, off:off + n]
        return v if dt == F32 else v.bitcast(dt)

    xs0 = carve(0, 4096).rearrange("p (k n) -> p k n", k=8)
    xs1 = sb(nc, "xs1", [128, 8, CH], F32)
    xs = xs0
    xss = [xs0, xs1]
    xb = [carve(4096 + 2048 * i, 2048, BF16).rearrange("p (k n) -> p k n", k=8) for i in range(2)]
    sq = carve(8192, 2048, BF16).rearrange("p (k n) -> p k n", k=8)
    R2 = sb(nc, "R2", [128, 12032], F32)
    a2 = Bump(R2)
    lnv = a2.take([128, CH], F32)
    rstd_bcs = [a2.take([128, CH], F32) for i in range(2)]
    rstd_col = a2.take([128, 4], F32)
    tm = carve(10240, 2048).rearrange("p (i n) -> p i n", i=4)
    qkb = a2.take([128, 4, 256], BF16)
    rt1 = a2.take([128, 4, 4, 16], F32)
    rt2 = a2.take([128, 4, 4, 16], F32)
    sga = a2.take([128, 4, 128], F32)
    UH = [a2.take([128, 16 + CH], F32) for i in range(2)]
    UHs = a2.take([128, 16, 48], F32)
    gpT = a2.take([128, CH], F32)
    sgp = a2.take([128, CH], F32)
    s2e = a2.take([128, 16 * 46], F32)
    s4e = a2.take([128, 16 * 44], F32)
    s8e = a2.take([128, 16 * 40], F32)
    s16 = a2.take([128, CH], F32)
    sel = a2.take([128, CH], F32)
    m_bf = a2.take([128, CH], BF16)
    pst = [a2.take([128, CH], BF16) for i in range(2)]
    UTp = a2.take([128, 128], F32)
    UTs = a2.take([128, 4, 128], F32)
    ucs = a2.take([128, CH], F32)

    wsem = P.new_dma_sem("wsem")
    t_w = P.dma("sp", wsem, xs[:, :, 0:512], w_tm.rearrange("(kc p) n -> p kc n", p=128))
    t_wt = None
    for kc in range(8):
        t_wt = P.op("dve", lambda e, kc=kc: e.tensor_scalar(Wtm[:, kc, :], xs[:, kc, :], gcol[:, kc:kc + 1], None, ALU.mult),
                    deps=[t_w, t_cst])
    t_w2 = P.dma("sp", wsem, xs[:, :, 0:256], w_fm.rearrange("(kc p) n -> p kc n", p=128), deps=[t_wt])
    for kc in range(8):
        t_wt = P.op("dve", lambda e, kc=kc: e.tensor_scalar(Wfm[:, kc, :], xs[:, kc, 0:256], gcol[:, kc:kc + 1], None, ALU.mult),
                    deps=[t_w2])
    t_wdone = t_wt

    t_uh0 = P.op("pool", lambda e: e.memset(UH[0][:, 0:16], 0.0))
    t_uhs0 = P.op("pool", lambda e: e.memset(UHs[:, :, 0:1], 0.0))
    spsem = P.new_dma_sem("spsem")
    t_sp = P.dma("sp", spsem, UHs[:, :, 1:16], io["spT"], deps=[t_uhs0])

    xsem = P.new_dma_sem("xsem")
    xsem2 = [P.new_dma_sem(f"xsem2_{i}") for i in range(2)]
    kvsem = P.new_dma_sem("kvsem")
    psem = [P.new_dma_sem(f"psem{i}") for i in range(2)]
    asem = [P.new_dma_sem(f"asem{i}") for i in range(2)]
    fsem = P.new_dma_sem("fsem")
    xTv = xT.rearrange("(kc p) n -> p kc n", p=128)
    k_out_v = io["k_out"].rearrange("(p n) e -> p n e", p=128)
    v_out_v = io["v_out"].rearrange("(p n) e -> p n e", p=128)
    exa_dst = io["exa_dst"]
    exp_dst = io["exp_dst"]

    psT = bank(4).bitcast(BF16).rearrange("p (w i e) -> p w i e", w=2, i=4)
    ps_ss = bank(0)
    ps_col = bank(1)[:, 0:4]
    ps_tm = [bank(2), bank(3)]
    ps_fm = [bank(5), bank(6)]
    ps_pool = bank(7)

    t_xfree = [t_wdone]
    xb_free = [[], []]
    tm_free = []
    qkb_free = []
    pst_free = [[], []]
    uh_tok = [t_uh0, None]
    ps_tm_free = [[], []]
    ps_fm_free = [[], []]
    psT_free = []
    ps_ss_free = []
    ps_col_free = []
    ps_pool_free = []
    rstd_free = []
    sq_free = []
    pool_tmp_free = []
    out_toks = []
    t_ktqt = None

    main_sec = P.cur
    SA0 = [P.section() for _ in range(NCH)]
    S1a = [P.section() for _ in range(NCH)]
    SA2 = [P.section() for _ in range(NCH)]
    S1b = [P.section() for _ in range(NCH)]
    SB2 = [P.section() for _ in range(NCH)]
    S2a = [P.section() for _ in range(NCH)]
    SC2 = [P.section() for _ in range(NCH)]
    S2b = [P.section() for _ in range(NCH)]
    rstd_free2 = [[], []]
    for t in range(NCH):
        is_s = (t == NCH - 1)
        xbt = xb[t % 2]
        rstd_bc = rstd_bcs[t % 2]
        rstd_free = rstd_free2[t % 2]
        P.use(SA0[t])
        xs = xss[t % 2]
        if t == 0:
            t_x = P.dma("sp", xsem2[0], xs, xTv[:, :, 0:CH], deps=t_xfree)
            t_xnext = P.dma("sp", xsem2[1], xss[1], xTv[:, :, CH:2 * CH])
        else:
            t_x = t_xnext
        t_cast = P.op("dve", lambda e, xbt=xbt: e.tensor_copy(xbt, xs), deps=[t_x] + xb_free[t % 2])
        t_sq = P.op("act", lambda e: e.activation(sq, xs, AF.Square), deps=[t_x] + sq_free)
        t_xfree = [t_cast, t_sq]
        if t >= 1:
            t_xnext = t_xnext2
        if t + 2 < NCH:
            t_xnext2 = P.dma("sp", xsem2[t % 2], xs, xTv[:, :, CH * (t + 2):CH * (t + 3)], deps=t_xfree)
        P.use(S1a[t])
        t_ss = None
        for kc in range(8):
            t_ss = P.op("pe", lambda e, kc=kc: e.matmul(ps_ss, lhsT=ones_bf, rhs=sq[:, kc, :], start=(kc == 0), stop=(kc == 7)),
                        deps=[t_sq, t_ones] + ps_ss_free, signal=(kc == 7))
        sq_free = [t_ss]
        t_ln = P.op("act", lambda e: e.activation(lnv, ps_ss, AF.Ln, bias=1e-6, scale=1.0 / D), deps=[t_ss] + rstd_free)
        ps_ss_free = [t_ln]
        t_rs = P.op("act", lambda e: e.activation(rstd_bc, lnv, AF.Exp, scale=-0.5), deps=[t_ln])
        P.use(SA2[t])
        t_cm = None
        for i in range(4):
            t_cm = P.op("pe", lambda e, i=i: e.matmul(ps_col[:, i:i + 1], lhsT=rstd_bc[0:1, 128 * i:128 * i + 128], rhs=one_f,
                                                      start=True, stop=True),
                        deps=[t_rs, t_onef] + ps_col_free, signal=(i == 3))
        t_rc = P.op("dve", lambda e: e.tensor_copy(rstd_col, ps_col), deps=[t_cm] + tm_free)
        ps_col_free = [t_rc]
        P.use(S1b[t])
        t_ev = []
        for i in range(4):
            pt = ps_tm[i % 2]
            t_mm = None
            for kc in range(8):
                t_mm = P.op("pe", lambda e, i=i, kc=kc, pt=pt: e.matmul(pt, lhsT=xbt[:, kc, 128 * i:128 * i + 128], rhs=Wtm[:, kc, :],
                                                                     start=(kc == 0), stop=(kc == 7)),
                            deps=[t_cast, t_wdone] + ps_tm_free[i % 2], signal=(kc == 7))
            te = P.op("act", lambda e, i=i, pt=pt: e.activation(tm[:, i, :], pt, AF.Copy, scale=rstd_col[:, i:i + 1]),
                      deps=[t_mm, t_rc] + tm_free)
            ps_tm_free[i % 2] = [te]
            t_ev.append(te)
        t_tm = t_ev[-1]
        qk = tm[:, :, 0:256].rearrange("p i (g d) -> p i g d", g=4)
        cc_b = CC[:, 4 * t:4 * t + 4, :].unsqueeze(2).to_broadcast([128, 4, 4, 16])
        ss_b = SS[:, 4 * t:4 * t + 4, :].unsqueeze(2).to_broadcast([128, 4, 4, 16])
        t_r1 = P.op("dve", lambda e: e.tensor_tensor(rt1, qk[:, :, :, 0:16], cc_b, ALU.mult), deps=[t_tm, t_cst])
        t_r2 = P.op("dve", lambda e: e.tensor_tensor(rt2[:, :, :, 0:8], qk[:, :, :, 8:16], ss_b[:, :, :, 0:8], ALU.mult), deps=[t_tm])
        t_r3 = P.op("dve", lambda e: e.tensor_tensor(rt2[:, :, :, 8:16], qk[:, :, :, 0:8], ss_b[:, :, :, 8:16], ALU.mult), deps=[t_r2])
        t_rope = P.op("dve", lambda e: e.tensor_tensor(qk[:, :, :, 0:16], rt1, rt2, ALU.add), deps=[t_r1, t_r3])
        t_ko = P.dma("sp", kvsem, k_out_v[:, 4 * t:4 * t + 4, :], tm[:, :, 128:256], deps=[t_rope])
        t_vo = P.dma("sp", kvsem, v_out_v[:, 4 * t:4 * t + 4, :], tm[:, :, 256:384], deps=[t_tm])
        t_qb = P.op("dve", lambda e: e.tensor_scalar(qkb[:, :, 0:128], tm[:, :, 0:128], 0.125, None, ALU.mult), deps=[t_rope] + qkb_free)
        t_kb = P.op("dve", lambda e: e.tensor_copy(qkb[:, :, 128:256], tm[:, :, 128:256]), deps=[t_qb])
        t_vb = P.op("dve", lambda e: e.tensor_copy(V1[:, 4 * t:4 * t + 4, 0:128], tm[:, :, 256:384]), deps=[t_tm])
        t_sg = P.op("act", lambda e: e.activation(sga, tm[:, :, 384:512], AF.Silu), deps=[t_tm])
        t_ga = P.op("dve", lambda e: e.tensor_tensor(GA[:, 4 * t:4 * t + 4, :], sga,
                                                      subln.unsqueeze(1).to_broadcast([128, 4, 128]), ALU.mult), deps=[t_sg, t_sub])
        tm_free = [t_ko, t_vo, t_kb, t_vb, t_sg]
        P.use(SB2[t])
        t_tr = None
        for w in range(2):
            for i in range(4):
                t_tr = P.op("pe", lambda e, w=w, i=i: e.transpose(psT[:, w, i, :], qkb[:, i, 128 * w:128 * w + 128], ident),
                            deps=[t_kb, t_id] + psT_free, signal=(w == 1 and i == 3))
        qkb_free = [t_tr]
        t_q = P.op("dve", lambda e: e.tensor_copy(QT[:, CH * t:CH * (t + 1)], psT[:, 0].rearrange("p i e -> p (i e)")), deps=[t_tr])
        t_k = P.op("dve", lambda e: e.tensor_copy(KT[:, CH * t:CH * (t + 1)], psT[:, 1].rearrange("p i e -> p (i e)")), deps=[t_q])
        psT_free = [t_k]
        t_ktqt = t_k
        P.use(S2a[t])
        t_fm = []
        for w in range(2):
            pf = ps_fm[w]
            t_mm = None
            for kc in range(8):
                t_mm = P.op("pe", lambda e, w=w, kc=kc, pf=pf: e.matmul(pf, lhsT=Wfm[:, kc, 128 * w:128 * w + 128], rhs=xbt[:, kc, :],
                                                                     start=(kc == 0), stop=(kc == 7)),
                            deps=[t_cast, t_wdone] + ps_fm_free[w], signal=(kc == 7))
            t_fm.append(t_mm)
        xb_free[t % 2] = [t_fm[1]]
        P.use(SC2[t])
        if not is_s:
            uh = UH[t % 2]
            S_, L_ = 1, CH
            uview = uh.rearrange("p (s l) -> p s l", s=1)
            t_u = P.op("dve", lambda e, uh=uh: e.tensor_tensor(uh[:, 16:16 + CH], ps_fm[0], rstd_bc, ALU.mult),
                       deps=[t_fm[0], t_rs, uh_tok[t % 2]] + pool_tmp_free)
        else:
            S_, L_ = 16, 32
            uview = UHs
            t_u = P.op("dve", lambda e: e.tensor_tensor(UHs[:, :, 16:48], ps_fm[0].rearrange("p (s l) -> p s l", s=16),
                                                        rstd_bc.rearrange("p (s l) -> p s l", s=16), ALU.mult),
                       deps=[t_fm[0], t_rs, t_sp] + pool_tmp_free)
        t_g = P.op("dve", lambda e: e.tensor_tensor(gpT, ps_fm[1], rstd_bc, ALU.mult), deps=[t_fm[1], t_rs] + pool_tmp_free)
        ps_fm_free = [[t_u], [t_g]]
        rstd_free2[t % 2] = [t_g, t_u, t_cm]
        t_sgp = P.op("act", lambda e: e.activation(sgp, gpT, AF.Silu), deps=[t_g])
        if t + 1 < NCH - 1:
            uh_tok[(t + 1) % 2] = P.op("dve", lambda e, t=t: e.tensor_copy(UH[(t + 1) % 2][:, 0:16], UH[t % 2][:, CH:CH + 16]), deps=[t_u])
        W2 = 14 + L_; W4 = 12 + L_; W8 = 8 + L_
        v2 = s2e[:, 0:S_ * W2].rearrange("p (s l) -> p s l", s=S_)
        v4 = s4e[:, 0:S_ * W4].rearrange("p (s l) -> p s l", s=S_)
        v8 = s8e[:, 0:S_ * W8].rearrange("p (s l) -> p s l", s=S_)
        v16 = s16.rearrange("p (s l) -> p s l", s=S_)
        vsel = sel.rearrange("p (s l) -> p s l", s=S_)
        vm = m_bf.rearrange("p (s l) -> p s l", s=S_)
        t_p = P.op("dve", lambda e: e.tensor_tensor(v2, uview[:, :, 2:16 + L_], uview[:, :, 1:15 + L_], ALU.add), deps=[t_u])
        t_p = P.op("dve", lambda e: e.tensor_tensor(v4, v2[:, :, 2:W2], v2[:, :, 0:W2 - 2], ALU.add), deps=[t_p])
        t_p = P.op("dve", lambda e: e.tensor_tensor(v8, v4[:, :, 4:W4], v4[:, :, 0:W4 - 4], ALU.add), deps=[t_p])
        t_p = P.op("dve", lambda e: e.tensor_tensor(v16, v8[:, :, 8:W8], v8[:, :, 0:W8 - 8], ALU.add), deps=[t_p])
        t_p = P.op("dve", lambda e: e.tensor_scalar(vsel, v2[:, :, 14:W2], meta[:, 0:1], None, ALU.mult), deps=[t_p, t_cst])
        t_p = P.op("dve", lambda e: e.scalar_tensor_tensor(vsel, v4[:, :, 12:W4], meta[:, 1:2], vsel, ALU.mult, ALU.add), deps=[t_p])
        t_p = P.op("dve", lambda e: e.scalar_tensor_tensor(vsel, v8[:, :, 8:W8], meta[:, 2:3], vsel, ALU.mult, ALU.add), deps=[t_p])
        t_p = P.op("dve", lambda e: e.scalar_tensor_tensor(vsel, v16, meta[:, 3:4], vsel, ALU.mult, ALU.add), deps=[t_p])
        if t == 0:
            t_p = P.op("dve", lambda e: e.tensor_tensor(sel[:, 0:16], sel[:, 0:16], meta[:, 8:24], ALU.mult), deps=[t_p])
        t_m = P.op("dve", lambda e: e.scalar_tensor_tensor(vm, vsel, meta[:, 4:5], uview[:, :, 16:16 + L_], ALU.mult, ALU.subtract),
                   deps=[t_p] + ps_pool_free)
        pool_tmp_free = [t_m]
        P.use(S2b[t])
        t_pm = P.op("pe", lambda e: e.matmul(ps_pool, lhsT=Wp, rhs=m_bf, start=True, stop=True), deps=[t_m, t_wp] + ps_pool_free)
        pso = pst[t % 2]
        t_po = P.op("dve", lambda e, pso=pso: e.scalar_tensor_tensor(pso, ps_pool, pscale[:, 0:1], sgp, ALU.mult, ALU.mult),
                    deps=[t_pm, t_sgp, t_cst] + pst_free[t % 2])
        ps_pool_free = [t_po, t_pm]
        t_pd = P.dma("sp", psem[t % 2], exp_dst(t), pso, deps=[t_po])
        pst_free[t % 2] = [t_pd]
        if t == NCH - 2:
            t_trp = P.op("pe", lambda e, t=t: e.matmul(ps_pool[:, 0:128], lhsT=UH[t % 2][:, 16 + 384:16 + 512], rhs=identf, start=True, stop=True),
                         deps=[t_u, t_idf] + ps_pool_free)
            t_cpo = P.op("act", lambda e: e.copy(UTp, ps_pool[:, 0:128]), deps=[t_trp])
            ps_pool_free = ps_pool_free + [t_cpo]
            out_toks.append(P.dma("sp", fsem, io["pool_out"][16], UTp[113:128, :], deps=[t_cpo]))
        if is_s:
            t_ucs = P.op("act", lambda e: e.copy(ucs.rearrange("p (s l) -> p s l", s=16), UHs[:, :, 16:48]), deps=[t_u])
            t_trp = None
            for g4 in range(4):
                t_trp = P.op("pe", lambda e, g4=g4: e.matmul(ps_pool[:, 128 * g4:128 * g4 + 128], lhsT=ucs[:, 128 * g4:128 * g4 + 128], rhs=identf, start=True, stop=True),
                             deps=[t_ucs, t_idf] + ps_pool_free, signal=(g4 == 3))
            t_cpo = P.op("act", lambda e: e.copy(UTs, ps_pool.rearrange("p (g c) -> p g c", g=4)), deps=[t_trp])
            ps_pool_free = ps_pool_free + [t_cpo]
            for s_ in range(16):
                k4, g4 = s_ % 4, s_ // 4
                out_toks.append(P.dma("sp", fsem, io["pool_out"][s_], UTs[32 * k4 + 17:32 * k4 + 32, g4, :], deps=[t_cpo]))
    P.use(main_sec)
    P.include(SA0[0]); P.include(S1a[0]); P.include(SA2[0]); P.include(SA0[1]); P.include(S1b[0])
    for t in range(1, NCH):
        P.include(S1a[t]); P.include(SB2[t - 1]); P.include(S2a[t - 1]); P.include(SA2[t])
        P.include(SC2[t - 1])
        if t + 1 < NCH:
            P.include(SA0[t + 1])
        P.include(S1b[t]); P.include(S2b[t - 1])
    L_ = NCH - 1
    P.include(SB2[L_]); P.include(S2a[L_]); P.include(SC2[L_]); P.include(S2b[L_])
    t_A_done = [t_ktqt, t_ga, t_vb]
    P.barrier()

    NFIN = 3
    fin_o1 = [sb(nc, f"fin_o1_{i}", [128, 128], F32) for i in range(NFIN)]
    fin_o = [sb(nc, f"fin_o_{i}", [128, 128], F32) for i in range(NFIN)]
    fin_sq = sb(nc, "fin_sq", [128, 128], F32)
    fin_rz = [sb(nc, f"fin_rz_{i}", [128, 4], F32) for i in range(NFIN)]
    fin_ss = [sb(nc, f"fin_ss_{i}", [128, 2], F32) for i in range(NFIN)]
    fin_state = {"n": 0, "free": [[] for _ in range(NFIN)]}

    def finalize(rows, O1, O2, ga_ap, dst_ap, deps, dst_deps, res):
        R = slice(0, rows)
        k = fin_state["n"] % NFIN
        fin_state["n"] += 1
        o1, o, rz, ss_ = fin_o1[k], fin_o[k], fin_rz[k], fin_ss[k]
        d0 = list(deps) + fin_state["free"][k]
        t1 = P.op("dve", lambda e: e.reciprocal(rz[R, 0:1], O1[:, 128:129]), deps=d0)
        t2 = P.op("dve", lambda e: e.reciprocal(rz[R, 1:2], O2[:, 128:129]), deps=d0)
        t3 = P.op("dve", lambda e: e.tensor_tensor(rz[R, 2:3], rz[R, 1:2], neglam[R, :], ALU.mult), deps=[t2, t_lam])
        t4 = P.op("dve", lambda e: e.tensor_scalar(o1[R, :], O1[:, 0:128], rz[R, 0:1], None, ALU.mult), deps=[t1])
        t5 = P.op("dve", lambda e: e.scalar_tensor_tensor(o[R, :], O2[:, 0:128], rz[R, 2:3], o1[R, :], ALU.mult, ALU.add),
                  deps=[t3, t4])
        res["acc"] = t5
        yield
        t6 = P.op("act", lambda e: e.activation(fin_sq[R, :], o[R, :], AF.Square, accum_out=ss_[R, 0:1]), deps=[t5])
        t7 = P.op("act", lambda e: e.activation(ss_[R, 1:2], ss_[R, 0:1], AF.Ln, bias=1e-5, scale=1.0 / 128), deps=[t6])
        t8 = P.op("act", lambda e: e.activation(ss_[R, 1:2], ss_[R, 1:2], AF.Exp, scale=-0.5), deps=[t7])
        yield
        t9 = P.op("dve", lambda e: e.scalar_tensor_tensor(dst_ap, o[R, :], ss_[R, 1:2], ga_ap, ALU.mult, ALU.mult),
                  deps=[t8] + list(dst_deps))
        fin_state["free"][k] = [t9]
        res["dst"] = t9
        res["tr"] = t9
        yield

    def run_all(gen):
        for _ in gen:
            pass

    t_ags = [None] * 5
    if do_sample:
        ck, cv = io["ck"], io["cv"]
        kc32 = [carve(2048 * i, 2048).rearrange("p (b e) -> p b e", b=16) for i in range(2)]
        vc32 = [carve(4096 + 2048 * i, 2048).rearrange("p (b e) -> p b e", b=16) for i in range(2)]
        kcb = carve(8192, 1024, BF16).rearrange("p (b e) -> p b e", b=16)
        KcT = [carve(9216 + 1040 * i, 1040, BF16) for i in range(2)]
        b2 = Bump(R2)
        Vc = [b2.take([128, 16, 132], BF16) for i in range(3)]
        PTz = b2.take([128, 16, 2, 128], BF16)
        PTn = [b2.take([128, 2, 128], BF16) for i in range(4)]
        aTs2 = b2.take([128, 4, 128], BF16)
        csem = [P.new_dma_sem(f"csem{i}") for i in range(2)]

        t_vc0 = [P.op("pool", lambda e, i=i: e.memset(Vc[i][:, :, 128:132], 0.0)) for i in range(3)]
        t_vc1 = [P.op("pool", lambda e, i=i: e.memset(Vc[i][:, :, 128:129], 1.0), deps=[t_vc0[i]]) for i in range(3)]
        t_z = P.op("pool", lambda e: e.memset(PTz, 0.0))
        t_zn = [P.op("pool", lambda e, i=i: e.memset(PTn[i], 0.0)) for i in range(4)]

        psK = big1[:, 0:512].bitcast(BF16).rearrange("p (b e) -> p b e", b=8)
        S_r = [big0[:, 0:1024], big0[:, 1024:2048]]
        S_new = [big1[:, 1024:1152], big1[:, 1536:1664]]
        acc_s = [big1[:, 512:642], big1[:, 768:898]]

        kc_free = [[], []]
        vc_free = [[], []]
        kcb_free = []
        psK_free = []
        kct_free = [[], []]
        vcb_free = [[], [], []]
        S_free = [[], []]
        Snew_free = []
        bmain = P.cur
        SX = [P.section() for _ in range(16)]
        SY1 = [P.section() for _ in range(16)]
        SY2 = [P.section() for _ in range(16)]
        ptn_free = [[] for _ in range(4)]
        acc_free_s = []
        loads = {}

        def issue_load(s):
            b_ = s % 2
            tk = P.dma("sp", csem[b_], kc32[b_], ck[s], deps=kc_free[b_] + t_A_done)
            tv = P.dma("sp", csem[b_], vc32[b_], cv[s], deps=vc_free[b_])
            loads[s] = (tk, tv)

        issue_load(0)
        t_last = None
        rnd_ctr = 0
        for s in range(16):
            b_ = s % 2
            k = s % 4
            g = s // 4
            tile = 64 + g
            tcol = NPR + 128 * g
            v3 = s % 3
            P.use(SX[s])
            if s + 1 < 16:
                issue_load(s + 1)
            tk, tv = loads[s]
            t_kb = P.op("dve", lambda e, b_=b_: e.tensor_copy(kcb, kc32[b_]), deps=[tk, tv] + kcb_free)
            kc_free[b_] = [t_kb]
            t_vb2 = P.op("act", lambda e, b_=b_, v3=v3: e.copy(Vc[v3][:, :, 0:128], vc32[b_]), deps=[tk, tv, t_vc1[v3]] + vcb_free[v3])
            vc_free[b_] = [t_vb2]
            for rnd in range(2):
                t_tr = None
                for blk in range(8):
                    t_tr = P.op("pe", lambda e, blk=blk, rnd=rnd: e.transpose(psK[:, blk, :], kcb[:, 8 * rnd + blk, :], ident),
                                deps=[t_kb, t_id] + psK_free, signal=(blk == 7))
                t_kt = P.op("act", lambda e, b_=b_, rnd=rnd: e.copy(KcT[b_][:, 1024 * rnd:1024 * rnd + 1024], psK.rearrange("p b e -> p (b e)")),
                            deps=[t_tr] + kct_free[b_])
                psK_free = [t_kt]
            kcb_free = [t_tr]
            P.use(SY1[s])
            t_e = None
            for r in range(4):
                par = rnd_ctr % 2
                rnd_ctr += 1
                Sv = S_r[par].rearrange("p (j b q) -> p j b q", j=2, b=4)
                t_s = None
                for bl in range(4):
                    blk = 4 * r + bl
                    for j in range(2):
                        t_s = P.op("pe", lambda e, bl=bl, blk=blk, j=j, b_=b_, Sv=Sv, tcol=tcol: e.matmul(
                            Sv[:, j, bl, :], lhsT=KcT[b_][64 * j:64 * j + 64, 128 * blk:128 * blk + 128],
                            rhs=QT[64 * j:64 * j + 64, tcol:tcol + 128], start=True, stop=True),
                            deps=[t_kt] + S_free[par] + t_A_done, signal=(bl == 3 and j == 1))
                for j in range(2):
                    t_e = P.op("act", lambda e, Sv=Sv, r=r, j=j, k=k: e.activation(
                        PTz[:, 4 * r:4 * r + 4, j, 32 * k:32 * k + 32], Sv[:, j, :, 32 * k:32 * k + 32], AF.Exp), deps=[t_s, t_z])
                S_free[par] = [t_e]
            t_sn = None
            for j in range(2):
                t_sn = P.op("pe", lambda e, j=j, tcol=tcol: e.matmul(
                    S_new[j], lhsT=KT[64 * j:64 * j + 64, tcol:tcol + 128],
                    rhs=QT[64 * j:64 * j + 64, tcol:tcol + 128], start=True, stop=True),
                    deps=Snew_free + t_A_done, signal=(j == 1))
            t_en = None
            for j in range(2):
                t_en = P.op("act", lambda e, k=k, j=j: e.activation(PTn[k][32 * k:32 * k + 32, j, 32 * k:32 * k + 32],
                                                                   S_new[j][32 * k:32 * k + 32, 32 * k:32 * k + 32], AF.Exp),
                            deps=[t_sn, t_zn[k]] + ptn_free[k])
            Snew_free = [t_en]
            kct_free[b_] = [t_s]
            P.use(SY2[s])
            t_pv = None
            for j in range(2):
                for blk in range(17):
                    first = (k == 0 and j == 0 and blk == 0)
                    lastm = (k == 3 and blk == 16)
                    if blk < 16:
                        fn = lambda e, j=j, blk=blk, v3=v3, first=first, lastm=lastm: e.matmul(
                            acc_s[j], lhsT=PTz[:, blk, j, :], rhs=Vc[v3][:, blk, 0:130],
                            start=first, stop=lastm, skip_group_check=True)
                    else:
                        fn = lambda e, j=j, k=k, tile=tile, first=first, lastm=lastm: e.matmul(
                            acc_s[j], lhsT=PTn[k][:, j, :], rhs=V1[:, tile, 0:130],
                            start=first, stop=lastm, skip_group_check=True)
                    t_pv = P.op("pe", fn, deps=[t_e, t_en, t_vb2] + (acc_free_s if k == 0 else []), signal=(j == 1 and blk == 16))
            vcb_free[v3] = [t_pv]
            ptn_free[k] = [t_pv]
            t_z = P.op("pool", lambda e, k=k: e.memset(PTz.rearrange("p b j q -> p (b j) q")[:, :, 32 * k:32 * k + 32], 0.0), deps=[t_pv])
            if k == 3:
                res = {}
                run_all(finalize(128, acc_s[0], acc_s[1], GA[:, tile, :], aTs2[:, g, :], [t_pv], [], res))
                acc_free_s = [res["acc"]]
                t_last = res["dst"]
        P.use(bmain)
        P.include(SX[0]); P.include(SX[1])
        for s in range(16):
            P.include(SY1[s])
            if s + 2 < 16:
                P.include(SX[s + 2])
            P.include(SY2[s])
        t_exs = P.dma("sp", fsem, exa_dst(16), aTs2, deps=[t_last])
        t_A_done = t_A_done + [t_last, t_pv]
        P.barrier()
        t_ags[4] = io["ag_fn"](4, [t_exs])
    if "pre_sec" in io:
        P.include(io["pre_sec"])

    PT = [carve(512 * i, 512, BF16).rearrange("p (j q) -> p j q", j=2) for i in range(3)]
    aT = [carve(1536 + 256 * i, 256, BF16).rearrange("p (i e) -> p i e", i=4) for i in range(2)]
    Sb = [big0[:, 0:1024].rearrange("p (j q) -> p j q", j=2), big0[:, 1024:2048].rearrange("p (j q) -> p j q", j=2)]

    def acc(j, qi):
        return big1[:, 512 * qi + 256 * j:512 * qi + 256 * j + 130]

    S_free = [[], []]
    PT_free = [[], [], []]
    acc_free = [[] for _ in range(4)]
    tr_free = [[] for _ in range(4)]
    aT_free = [[], []]
    steps = [(t, kb) for t in range(nchunks_attn) for kb in range(4 * t + 4)]

    def emit_qk(i):
        t, kb = steps[i]
        c0 = max(0, 128 * (kb - 4 * t))
        Sv = Sb[i % 2]
        t_s = None
        for j in range(2):
            t_s = P.op("pe", lambda e, j=j, kb=kb, c0=c0, Sv=Sv, t=t: e.matmul(
                Sv[:, j, c0:CH], lhsT=KT[64 * j:64 * j + 64, 128 * kb:128 * kb + 128],
                rhs=QT[64 * j:64 * j + 64, CH * t + c0:CH * (t + 1)], start=True, stop=True),
                deps=S_free[i % 2] + t_A_done, signal=(j == 1))
        return t_s

    t_ad_last = [None, None]
    DA, DT = 8, 10
    fb = Bump(R2)
    fb.off = 10600
    fo = [[fb.take([128, 128], F32) for _ in range(4)] for _ in range(2)]
    fo1 = fb.take([128, 128], F32)
    fsq = fb.take([128, 128], F32)
    frz = fb.take([128, 2, 4, 4], F32)
    fss = fb.take([128, 2, 4], F32)
    fln = fb.take([128, 2, 4], F32)
    frs = fb.take([128, 2, 4], F32)
    fr_free = [[], []]
    ta_tok = [[], []]
    last_dve = {"t5": [], "t7": []}
    pendA = []
    pendT = []

    def emit_act_stage(tt, toks):
        par = tt % 2
        tA = P.op("act", lambda e, par=par: e.activation(fln[:, par, :], fss[:, par, :], AF.Ln, bias=1e-5, scale=1.0 / 128),
                  deps=toks + fr_free[par])
        ta_tok[par] = [tA]
        return P.op("act", lambda e, par=par: e.activation(frs[:, par, :], fln[:, par, :], AF.Exp, scale=-0.5), deps=[tA])

    def emit_tail(tt, tB):
        par = tt % 2
        t9 = None
        for qi in range(4):
            t9 = P.op("dve", lambda e, par=par, qi=qi, tt=tt: e.scalar_tensor_tensor(
                aT[par][:, qi, :], fo[par][qi], frs[:, par, qi:qi + 1], GA[:, 4 * tt + qi, :], ALU.mult, ALU.mult),
                deps=[tB] + (aT_free[par] if qi == 0 else []))
        fr_free[par] = [t9]
        t_ad = P.dma("sp", asem[par], exa_dst(tt), aT[par], deps=[t9])
        aT_free[par] = [t_ad]
        t_ad_last[par] = t_ad
        if tt % 4 == 3:
            prev = [x for x in t_ags if x is not None]
            t_ags[tt // 4] = io["ag_fn"](tt // 4, [t_ad_last[0], t_ad_last[1]] + prev)
            if tt == 7 and "pre2_sec" in io:
                P.include(io["pre2_sec"])

    t_s_next = emit_qk(0) if steps else None
    for i, (t, kb) in enumerate(steps):
        r = kb - 4 * t
        c0 = max(0, 128 * r)
        Sv = Sb[i % 2]
        ptb = PT[i % 3]
        t_s = t_s_next
        if i + 1 < len(steps):
            t_s_next = emit_qk(i + 1)
        t_e = P.op("act", lambda e, ptb=ptb, Sv=Sv, c0=c0: e.activation(ptb[:, :, c0:CH], Sv[:, :, c0:CH], AF.Exp),
                   deps=[t_s] + PT_free[i % 3])
        S_free[i % 2] = [t_e]
        for rec in [x for x in pendA if i >= x[1] + DA]:
            pendA.remove(rec)
            pendT.append([rec[0], rec[1], emit_act_stage(rec[0], rec[2])])
        t_m = t_e
        if r >= 0:
            t_m = P.op("pool", lambda e, ptb=ptb, c0=c0: e.memset(ptb[64:128, :, c0:c0 + 64], 0.0), deps=[t_e])
        t_pv = None
        qis = list(range(max(r, 0), 4))
        for qi in qis:
            for j in range(2):
                last = (qi == qis[-1] and j == 1)
                t_pv = P.op("pe", lambda e, j=j, qi=qi, kb=kb, ptb=ptb, t=t: e.matmul(
                    acc(j, qi), lhsT=ptb[:, j, 128 * qi:128 * qi + 128], rhs=V1[:, kb, 0:130],
                    start=(kb == 0 and j == 0), stop=(kb == 4 * t + qi), skip_group_check=True),
                    deps=[t_m] + (acc_free[qi] if kb == 0 else []), signal=last)
        PT_free[i % 3] = [t_pv]
        for rec in [x for x in pendT if i >= x[1] + DT]:
            pendT.remove(rec)
            emit_tail(rec[0], rec[2])
        if r >= 0:
            qi = r
            par = t % 2
            O1, O2 = acc(0, qi), acc(1, qi)
            rz = frz[:, par, qi, :]
            o = fo[par][qi]
            t1 = P.op("dve", lambda e, rz=rz, O1=O1: e.reciprocal(rz[:, 0:1], O1[:, 128:129]), deps=[t_pv] + fr_free[par])
            t2 = P.op("dve", lambda e, rz=rz, O2=O2: e.reciprocal(rz[:, 1:2], O2[:, 128:129]), deps=[t_pv] + fr_free[par])
            t3 = P.op("dve", lambda e, rz=rz: e.tensor_tensor(rz[:, 2:3], rz[:, 1:2], neglam, ALU.mult), deps=[t2, t_lam])
            t4 = P.op("dve", lambda e, rz=rz, O1=O1: e.tensor_scalar(fo1, O1[:, 0:128], rz[:, 0:1], None, ALU.mult),
                      deps=[t1] + last_dve["t5"])
            t5 = P.op("dve", lambda e, rz=rz, O2=O2, o=o: e.scalar_tensor_tensor(o, O2[:, 0:128], rz[:, 2:3], fo1, ALU.mult, ALU.add),
                      deps=[t3, t4] + fr_free[par])
            last_dve["t5"] = [t5]
            acc_free[qi] = [t5]
            if qi == 3:
                t7 = None
                for q2 in range(4):
                    t6 = P.op("dve", lambda e, par=par, q2=q2: e.tensor_tensor(fsq, fo[par][q2], fo[par][q2], ALU.mult),
                              deps=[t5] + last_dve["t7"])
                    t7 = P.op("dve", lambda e, par=par, q2=q2: e.tensor_reduce(fss[:, par, q2:q2 + 1], fsq, mybir.AxisListType.X, ALU.add),
                              deps=[t6] + ta_tok[par])
                    last_dve["t7"] = [t7]
                pendA.append([t, i, [t7]])
    for rec in pendA:
        pendT.append([rec[0], rec[1], emit_act_stage(rec[0], rec[2])])
    for rec in pendT:
        emit_tail(rec[0], rec[2])
    return {"R2": R2, "t_ags": t_ags, "QT": QT, "KT": KT, "V1": V1, "GA": GA, "carve": carve, "big0": big0, "big1": big1, "ident": ident, "t_id": t_id}


def emit_phase_c2(nc, P, io, Gs, t_ags, loc2s, G2s, H):
    carve, big0, big1, ident, t_id = H["carve"], H["big0"], H["big1"], H["ident"], H["t_id"]
    Gv_a = [g_.rearrange("(r h p n) e -> p r h n e", r=4, h=2, p=128) for g_ in Gs]
    Gv_p = [g_.rearrange("(r h c n) e -> c r h n e", r=4, h=2, c=128) for g_ in Gs]
    c2b = Bump(H["R2"])
    NLB = 3
    Aload = [c2b.take([128, 4, 4, 128], BF16) for i in range(NLB)]
    ATp = [c2b.take([128, 4, 4, 128], BF16) for i in range(NLB)]
    ATa = [carve(4096 + 1024 * i, 1024, BF16).rearrange("p (r n e) -> p r n e", r=4, n=4) for i in range(2)]
    xr = [c2b.take([128, 4, 256], F32) for i in range(NLB)]
    wo32 = carve(8192, 2048).rearrange("p (k d) -> p k d", k=8)
    Wo = carve(10240, 1024, BF16).rearrange("p (k d) -> p k d", k=8)
    fgj = carve(11264, 256)
    junk = carve(11520, 256)
    ss = carve(11776, 68)
    gs = carve(11844, 272).rearrange("p (r c) -> p r c", r=4)
    rstd = carve(12116, 68)
    lnt = carve(12184, 68)
    yk = []
    yreg = []
    for nm in ("QT", "KT", "V1", "GA"):
        t_ = H[nm]
        flat = t_ if len(t_.shape) == 2 else t_.rearrange("p n e -> p (n e)")
        f32v = flat.bitcast(F32)
        yreg.append(f32v)
        for i in range(17):
            yk.append(f32v[:, 256 * i:256 * i + 256])
    yv4 = io["y_out"].rearrange("(p m) d -> p m d", p=128)
    pso = [big0[:, 0:1024].rearrange("p (n d) -> p n d", n=4), big0[:, 1024:2048].rearrange("p (n d) -> p n d", n=4)]
    psA = [big1[:, 1024 * i:1024 * i + 1024].bitcast(BF16).rearrange("p (r n e) -> p r n e", r=4, n=4) for i in range(2)]

    wsem = P.new_dma_sem("c_wsem")
    gl = [P.new_dma_sem(f"c_gl{i}") for i in range(NLB)]
    xsm = [P.new_dma_sem(f"c_xs{i}") for i in range(NLB)]
    ysem = P.new_dma_sem("c_ysem")
    cur_ = P.cur
    if "pre_sec" in io:
        P.use(io["pre_sec"])
    P.dma("sp", wsem, wo32, io["w_out_j"].rearrange("(k p) n -> p k n", p=128))
    t_w = P.dma("sp", wsem, fgj, io["fgj"])
    P.use(cur_)
    t_wo = P.op("dve", lambda e: e.tensor_copy(Wo, wo32), deps=[t_w])
    xv = io["xres"].rearrange("(p m) d -> p m d", p=128)
    yv = io["y_out"].rearrange("(p m) d -> m p d", p=128)
    al_free = [[] for _ in range(NLB)]
    atp_free = [[] for _ in range(NLB)]
    ata_free = [[], []]
    xr_free = [[] for _ in range(NLB)]
    psA_free = [[], []]
    pso_free = [[], []]
    t_sq = None
    ssems = [P.new_dma_sem(f"c_ssem{i}") for i in range(4)]
    cc_prev = []
    cmain = P.cur
    Nsec = [P.section() for _ in range(2)]
    for t in range(NCH):
        b_ = t % 2
        l_ = t % NLB
        q_ = min(t // 4, 4)
        lt = t % 4 if t < 16 else 0
        t_ag = t_ags[q_]
        sec_ = P.cur
        if t < NLB and "pre2_sec" in io:
            P.use(io["pre2_sec"])
        for r in range(4):
            t_la = P.dma("sp", gl[l_], Aload[l_][:, r], Gv_a[q_][:, r, 0, 4 * lt:4 * lt + 4, :], deps=[t_ag] + al_free[l_])
        for r in range(4):
            t_lp = P.dma("sp", gl[l_], ATp[l_][:, r], Gv_p[q_][:, r, 1, 4 * lt:4 * lt + 4, :], deps=[t_ag] + atp_free[l_])
        t_lx = P.dma("sp", xsm[l_], xr[l_], xv[:, 4 * t:4 * t + 4, :], deps=xr_free[l_])
        P.use(sec_)
        t_tr = None
        for r in range(4):
            for n in range(4):
                t_tr = P.op("pe", lambda e, r=r, n=n, b_=b_, l_=l_: e.transpose(psA[b_][:, r, n, :], Aload[l_][:, r, n, :], ident),
                            deps=[t_la, t_lp, t_id] + psA_free[b_], signal=(r == 3 and n == 3))
        al_free[l_] = [t_tr]
        t_cp = P.op("act", lambda e, b_=b_: e.copy(ATa[b_].rearrange("p r n e -> p (r n e)"), psA[b_].rearrange("p r n e -> p (r n e)")),
                    deps=[t_tr] + ata_free[b_])
        psA_free[b_] = [t_cp]
        t_mm = None
        for n in range(4):
            for k in range(8):
                src = ATa[b_] if k < 4 else ATp[l_]
                t_mm = P.op("pe", lambda e, n=n, k=k, b_=b_, src=src: e.matmul(
                    pso[b_][:, n, :], lhsT=src[:, k % 4, n, :], rhs=Wo[:, k, :], start=(k == 0), stop=(k == 7)),
                    deps=[t_cp, t_lp, t_wo] + pso_free[b_], signal=(n == 3 and k == 7))
        ata_free[b_] = [t_mm]
        atp_free[l_] = [t_mm]
        t_y = None
        for n in range(4):
            yk_ = yk[4 * t + n]
            t_y = P.op("dve", lambda e, n=n, b_=b_, l_=l_, yk_=yk_: e.tensor_tensor(yk_, pso[b_][:, n, :], xr[l_][:, n, :], ALU.add),
                       deps=[t_mm, t_lx])
            t_sq = P.op("act", lambda e, yk_=yk_, i_=4 * t + n: e.activation(junk, yk_, AF.Square, accum_out=ss[:, i_:i_ + 1]),
                        deps=[t_y] + ([t_sq] if t_sq is not None else []))
        pso_free[b_] = [t_y]
        xr_free[l_] = [t_y]
        if t == 11 or t == NCH - 1:
            q_n = 0 if t == 11 else 1
            lo, hi = (0, 48) if q_n == 0 else (48, NTILE)
            t_sd = P.dma("sp", ssems[2 * q_n], loc2s[q_n], ss[:, lo:hi], deps=[t_sq])
            t_ag2 = P.coll(lambda e, q_n=q_n: e.collective_compute("AllGather", ALU.bypass, replica_groups=[[0, 1, 2, 3], [4, 5, 6, 7]],
                                                                  ins=[loc2s[q_n].opt()], outs=[G2s[q_n].opt()]),
                           deps=[t_sd] + [x for x in t_ags if x is not None] + cc_prev)
            P.use(Nsec[q_n])
            cc_prev = [t_ag2]
            t_g = P.dma("sp", ssems[2 * q_n + 1], gs[:, :, lo:hi], G2s[q_n].rearrange("(r p) c -> p r c", p=128), deps=[t_ag2, t_sd])
            t_a = P.op("dve", lambda e, lo=lo, hi=hi: e.tensor_tensor(rstd[:, lo:hi], gs[:, 0, lo:hi], gs[:, 1, lo:hi], ALU.add), deps=[t_g])
            t_a = P.op("dve", lambda e, lo=lo, hi=hi: e.tensor_tensor(rstd[:, lo:hi], rstd[:, lo:hi], gs[:, 2, lo:hi], ALU.add), deps=[t_a])
            t_a = P.op("dve", lambda e, lo=lo, hi=hi: e.tensor_tensor(rstd[:, lo:hi], rstd[:, lo:hi], gs[:, 3, lo:hi], ALU.add), deps=[t_a])
            t_a = P.op("act", lambda e, lo=lo, hi=hi: e.activation(lnt[:, lo:hi], rstd[:, lo:hi], AF.Ln, bias=1e-6, scale=1.0 / D), deps=[t_a])
            t_a = P.op("act", lambda e, lo=lo, hi=hi: e.activation(rstd[:, lo:hi], lnt[:, lo:hi], AF.Exp, scale=-0.5), deps=[t_a])
            i = lo
            while i < hi:
                reg, j0 = i // 17, i % 17
                n_ = min(4, 17 - j0, hi - i)
                t_o = None
                for k_ in range(n_):
                    t_o = P.op("dve", lambda e, i_=i + k_: e.scalar_tensor_tensor(yk[i_], yk[i_], rstd[:, i_:i_ + 1], fgj, ALU.mult, ALU.mult),
                               deps=[t_a, t_w])
                P.dma("sp", ysem, yv4[:, i:i + n_, :], yreg[reg][:, 256 * j0:256 * (j0 + n_)].rearrange("p (n d) -> p n d", n=n_), deps=[t_o])
                i += n_
            P.use(cmain)
    P.use(cmain)
    P.include(Nsec[0])
    P.include(Nsec[1])
    return ysem


def build_fused(nchunks_attn=16, do_sample=True):
    nc = bass.Bass("TRN2", target_bir_lowering=False)
    io = {}

    def inp(name, shape, dt=F32):
        io[name] = nc.dram_tensor(name, shape, dt, kind="ExternalInput").ap()

    def outp(name, shape, dt=F32):
        io[name] = nc.dram_tensor(name, shape, dt, kind="ExternalOutput").ap()

    inp("xT", [D, NT]); inp("w_tm", [D, 512]); inp("w_fm", [D, 256])
    inp("gcol", [128, 8]); inp("subln", [128, 128]); inp("pscale", [128, 1]); inp("meta", [128, 32])
    inp("lamv", [128, 4, 64]); inp("wpool", [128, 128]); inp("cc", [NT, 16]); inp("ss", [NT, 16])
    inp("spT", [128, 16, 15]); inp("ck", [16, 128, 16, 128]); inp("cv", [16, 128, 16, 128])
    inp("w_out_j", [D, 256]); inp("fgj", [128, 256]); inp("xres", [NT, 256])
    outp("k_out", [NT, 128]); outp("v_out", [NT, 128]); outp("pool_out", [17, 15, 128])
    outp("y_out", [NT, 256])
    locs = [nc.dram_tensor(f"xloc{q}", [4096, 128], BF16).ap() for q in range(4)] + [nc.dram_tensor("xloc4", [1024, 128], BF16).ap()]
    Gs = [nc.dram_tensor(f"xg{q}", [4 * 4096, 128], BF16).ap() for q in range(4)] + [nc.dram_tensor("xg4", [4 * 1024, 128], BF16).ap()]
    loc2 = [nc.dram_tensor(f"xss_loc{q}", [128, 48 if q == 0 else 20], F32).ap() for q in range(2)]
    G2 = [nc.dram_tensor(f"xss_g{q}", [4 * 128, 48 if q == 0 else 20], F32).ap() for q in range(2)]

    def exa_dst(t):
        if t < 16:
            return locs[t // 4][0:2048, :].rearrange("(p n) e -> p n e", p=128)[:, 4 * (t % 4):4 * (t % 4) + 4, :]
        return locs[4][0:512, :].rearrange("(p n) e -> p n e", p=128)

    def exp_dst(t):
        if t < 16:
            return locs[t // 4][2048:4096, :].rearrange("(c n) e -> c (n e)", c=128)[:, 512 * (t % 4):512 * (t % 4) + 512]
        return locs[4][512:1024, :].rearrange("(c n) e -> c (n e)", c=128)

    io["exa_dst"] = exa_dst
    io["exp_dst"] = exp_dst
    P = Prog(nc)
    GR = [[0, 1, 2, 3], [4, 5, 6, 7]]

    def ag_fn(q, deps):
        return P.coll(lambda e, q=q: e.collective_compute("AllGather", ALU.bypass, replica_groups=GR,
                                                          ins=[locs[q].opt()], outs=[Gs[q].opt()]), deps=deps)

    io["ag_fn"] = ag_fn
    io["pre_sec"] = P.section()
    io["pre2_sec"] = P.section()
    H = emit_phase_ab(nc, P, io, nchunks_attn, do_sample)
    P.barrier()
    t_ags = H["t_ags"]
    emit_phase_c2(nc, P, io, Gs, t_ags, loc2, G2, H)
    P.wait_all_dma("sp")
    P.finish()
    return nc


def _rope_tables():
    inv = (np.float32(500000.0) ** (-np.arange(0, 16, 2, dtype=np.float32) / np.float32(16))).astype(np.float32)
    pos = np.concatenate([np.arange(NPR), np.tile(PAST + np.arange(32), 16)]).astype(np.float32)
    ang = (pos[:, None] * inv[None, :]).astype(np.float32)
    c = np.cos(ang).astype(np.float32)
    s = np.sin(ang).astype(np.float32)
    return np.ascontiguousarray(np.concatenate([c, c], 1)), np.ascontiguousarray(np.concatenate([-s, s], 1))


def prep_ab(inp):
    cc, ss = _rope_tables()
    maps = []
    w_in = inp["w_in"][0]
    for c in range(8):
        b, h = divmod(c, 4)
        xs = inp["x_sample"][16 * b:16 * b + 16].reshape(NSM, D)
        xT = np.ascontiguousarray(np.concatenate([inp["x_prompt"][b], xs], 0).T)
        cols = lambda base: w_in[:, base + 128 * h: base + 128 * h + 128]
        w_tm = np.ascontiguousarray(np.concatenate([cols(0), cols(512), cols(1024), cols(1536)], 1))
        w_fm = np.ascontiguousarray(np.concatenate([cols(2048), cols(2560)], 1))
        w = POOL_WINDOWS[h]
        meta = np.zeros((128, 32), np.float32)
        meta[:, h] = 1.0
        meta[:, 4] = 1.0 / w
        meta[:, 8:24] = (w / np.minimum(np.arange(16) + 1, w)).astype(np.float32)[None, :]
        lamv = np.stack([inp["lambda_q1"][0], inp["lambda_k1"][0], inp["lambda_q2"][0], inp["lambda_k2"][0]], 0)
        maps.append({
            "xT": xT, "w_tm": w_tm, "w_fm": w_fm,
            "gcol": np.ascontiguousarray(inp["norm_g"][0].reshape(8, 128).T),
            "subln": np.ascontiguousarray(np.broadcast_to(inp["subln_g"][0][None, :], (128, 128))),
            "pscale": np.ascontiguousarray(inp["pool_scale"][0][128 * h:128 * h + 128].reshape(128, 1)),
            "meta": meta,
            "lamv": np.ascontiguousarray(np.broadcast_to(lamv[None], (128, 4, 64))),
            "wpool": np.ascontiguousarray(inp["w_pool"][0, h]),
            "cc": cc, "ss": ss,
            "spT": np.ascontiguousarray(inp["state_pool"][0, 16 * b:16 * b + 16, :, 128 * h:128 * h + 128].transpose(2, 0, 1)),
            "ck": np.ascontiguousarray(inp["cache_k"][0, 16 * b:16 * b + 16, :, h, :].reshape(16, 16, 128, 128).transpose(0, 2, 1, 3)),
            "cv": np.ascontiguousarray(inp["cache_v"][0, 16 * b:16 * b + 16, :, h, :].reshape(16, 16, 128, 128).transpose(0, 2, 1, 3)),
        })
    return maps


def kernel(**inputs):
    inp = {k: np.asarray(v) for k, v in inputs.items()}
    maps = prep_ab(inp)
    for c in range(8):
        b, j = divmod(c, 4)
        xs = inp["x_sample"][16 * b:16 * b + 16].reshape(NSM, D)
        maps[c]["w_out_j"] = np.ascontiguousarray(inp["w_out"][0][:, 256 * j:256 * j + 256])
        maps[c]["fgj"] = np.ascontiguousarray(np.broadcast_to(inp["final_g"][None, 256 * j:256 * j + 256], (128, 256)))
        xr_ = np.concatenate([inp["x_prompt"][b][:, 256 * j:256 * j + 256], xs[:, 256 * j:256 * j + 256]], 0)
        maps[c]["xres"] = np.ascontiguousarray(xr_.reshape(NTILE, 128, 256).transpose(1, 0, 2)).reshape(NT, 256)
    nc = build_fused()
    res = run_bass_kernel_spmd(nc, maps, core_ids=list(range(8))).results
    return assemble(res, res)


def assemble(res1, res2):
    y_prompt = np.empty((2, NPR, D), np.float32)
    y_sample = np.empty((32, 32, D), np.float32)
    k_prompt = np.empty((1, 2, NPR, 4, 128), np.float32)
    v_prompt = np.empty((1, 2, NPR, 4, 128), np.float32)
    pool_prompt = np.empty((1, 2, 15, 512), np.float32)
    k_sample = np.empty((1, 32, 32, 4, 128), np.float32)
    v_sample = np.empty((1, 32, 32, 4, 128), np.float32)
    pool_sample = np.empty((1, 32, 15, 512), np.float32)
    for c in range(8):
        b, h = divmod(c, 4)
        r = dict(res1[c])
        for nm_, w_ in (("k_out", 128), ("v_out", 128), ("y_out", 256)):
            if nm_ in r:
                r[nm_] = np.asarray(r[nm_]).reshape(128, NTILE, w_).transpose(1, 0, 2).reshape(NT, w_)
        k_prompt[0, b, :, h, :] = r["k_out"][:NPR]
        v_prompt[0, b, :, h, :] = r["v_out"][:NPR]
        k_sample[0, 16 * b:16 * b + 16, :, h, :] = r["k_out"][NPR:].reshape(16, 32, 128)
        v_sample[0, 16 * b:16 * b + 16, :, h, :] = r["v_out"][NPR:].reshape(16, 32, 128)
        pool_prompt[0, b, :, 128 * h:128 * h + 128] = r["pool_out"][16]
        pool_sample[0, 16 * b:16 * b + 16, :, 128 * h:128 * h + 128] = r["pool_out"][:16]
        if res2 is not None:
            y = r["y_out"]
            y_prompt[b, :, 256 * h:256 * h + 256] = y[:NPR]
            y_sample[16 * b:16 * b + 16, :, 256 * h:256 * h + 256] = y[NPR:].reshape(16, 32, 256)
    return (y_prompt, y_sample, k_prompt, v_prompt, pool_prompt, k_sample, v_sample, pool_sample)
```
